# Optimizing a Trainium2 kernel written in Bass

```python
import math
import jax, jax.numpy as jnp
from jax import lax
import numpy as np

D_MODEL = 1024
BATCH = 4
SEQ = 8192
DEPTH = 2

HEAD_DIM = 64
N_FOX_HEADS = 8
N_MOBA_HEADS = 8
FOX_WIDTH = N_FOX_HEADS * HEAD_DIM
MOBA_WIDTH = N_MOBA_HEADS * HEAD_DIM
MIX_WIDTH = FOX_WIDTH + MOBA_WIDTH
IN_COLS = 3 * FOX_WIDTH + N_FOX_HEADS + 3 * MOBA_WIDTH
FOX_Q_BLOCK = 128
MOBA_BLOCK = 256
MOBA_TOPK = 3
MOBA_Q_CHUNK = 64
D_FF = -(-8 * D_MODEL // (3 * 256)) * 256
ALPHA = (2 * DEPTH) ** 0.25
BETA = (8 * DEPTH) ** -0.25
LN_EPS = 1e-5
NEG_INF = -1e30

kernel_name = "fox_moba_hybrid_deepnorm"


def layer_norm(x, g, b):
    xf = x.astype(jnp.float32)
    mu = xf.mean(-1, keepdims=True)
    var = jnp.square(xf - mu).mean(-1, keepdims=True)
    y = (xf - mu) * lax.rsqrt(var + LN_EPS)
    return (y * g + b).astype(x.dtype)


def split_heads(t, n_heads):
    B, S, _ = t.shape
    return t.reshape(B, S, n_heads, HEAD_DIM).transpose(0, 2, 1, 3)


def alibi_slopes(n_heads):
    return jnp.asarray([2.0 ** (-8.0 * (i + 1) / n_heads) for i in range(n_heads)], dtype=jnp.float32)


def fox_attention(q, k, v, log_f):
    B, H, S, _ = q.shape
    scale = HEAD_DIM ** -0.5
    c = lax.cumsum(log_f, axis=2)
    n_blocks = S // FOX_Q_BLOCK
    q_blk = q.reshape(B, H, n_blocks, FOX_Q_BLOCK, HEAD_DIM).transpose(2, 0, 1, 3, 4)
    c_blk = c.reshape(B, H, n_blocks, FOX_Q_BLOCK).transpose(2, 0, 1, 3)
    k_pos = jnp.arange(S)

    def one_block(args):
        i, qb, cb = args
        q_pos = i * FOX_Q_BLOCK + jnp.arange(FOX_Q_BLOCK)
        s = jnp.einsum('bhqd,bhkd->bhqk', qb, k).astype(jnp.float32) * scale
        s = s + cb[..., :, None] - c[..., None, :]
        s = jnp.where(k_pos[None, :] <= q_pos[:, None], s, NEG_INF)
        p = jax.nn.softmax(s, axis=-1)
        return jnp.einsum('bhqk,bhkd->bhqd', p.astype(v.dtype), v)

    out = lax.map(one_block, (jnp.arange(n_blocks), q_blk, c_blk))
    return out.transpose(1, 2, 0, 3, 4).reshape(B, H, S, HEAD_DIM)


def moba_attention(q, k, v, slopes):
    B, H, S, _ = q.shape
    scale = HEAD_DIM ** -0.5
    n_kb = -(-S // MOBA_BLOCK)
    pad = n_kb * MOBA_BLOCK - S
    k_p = jnp.pad(k, ((0, 0), (0, 0), (0, pad), (0, 0)))
    v_p = jnp.pad(v, ((0, 0), (0, 0), (0, pad), (0, 0)))
    k_blocks = k_p.reshape(B, H, n_kb, MOBA_BLOCK, HEAD_DIM)
    v_blocks = v_p.reshape(B, H, n_kb, MOBA_BLOCK, HEAD_DIM)
    k_mean = k_blocks.astype(jnp.float32).mean(axis=3).astype(k.dtype)
    top_k = min(MOBA_TOPK, n_kb)
    n_chunks = S // MOBA_Q_CHUNK
    q_chunks = q.reshape(B, H, n_chunks, MOBA_Q_CHUNK, HEAD_DIM).transpose(2, 0, 1, 3, 4)
    blk_ids = jnp.arange(n_kb)
    in_blk = jnp.arange(MOBA_BLOCK)
    b_idx = jnp.arange(B)[:, None, None, None]
    h_idx = jnp.arange(H)[None, :, None, None]

    def one_chunk(args):
        i, qc = args
        q_pos = i * MOBA_Q_CHUNK + jnp.arange(MOBA_Q_CHUNK)
        own = (i * MOBA_Q_CHUNK) // MOBA_BLOCK
        g = jnp.einsum('bhqd,bhnd->bhqn', qc, k_mean).astype(jnp.float32)
        g = jnp.where(blk_ids < own, g, NEG_INF)
        _, sel = lax.top_k(g, top_k)
        valid = sel < own
        k_sel = k_blocks[b_idx, h_idx, sel]
        v_sel = v_blocks[b_idx, h_idx, sel]
        s_sel = jnp.einsum('bhqd,bhqnkd->bhqnk', qc, k_sel).astype(jnp.float32) * scale
        pos_sel = sel[..., None] * MOBA_BLOCK + in_blk
        dist_sel = (q_pos[None, None, :, None, None] - pos_sel).astype(jnp.float32)
        s_sel = s_sel - slopes[None, :, None, None, None] * dist_sel
        s_sel = jnp.where(valid[..., None], s_sel, NEG_INF)
        k_own = lax.dynamic_index_in_dim(k_blocks, own, axis=2, keepdims=False)
        v_own = lax.dynamic_index_in_dim(v_blocks, own, axis=2, keepdims=False)
        pos_own = own * MOBA_BLOCK + in_blk
        s_own = jnp.einsum('bhqd,bhkd->bhqk', qc, k_own).astype(jnp.float32) * scale
        dist_own = (q_pos[:, None] - pos_own[None, :]).astype(jnp.float32)
        s_own = s_own - slopes[None, :, None, None] * dist_own
        s_own = jnp.where(pos_own[None, :] <= q_pos[:, None], s_own, NEG_INF)
        n_sel = top_k * MOBA_BLOCK
        s_all = jnp.concatenate([s_sel.reshape(B, H, MOBA_Q_CHUNK, n_sel), s_own], axis=-1)
        p = jax.nn.softmax(s_all, axis=-1).astype(v.dtype)
        p_sel = p[..., :n_sel].reshape(B, H, MOBA_Q_CHUNK, top_k, MOBA_BLOCK)
        p_own = p[..., n_sel:]
        return (jnp.einsum('bhqnk,bhqnkd->bhqd', p_sel, v_sel)
                + jnp.einsum('bhqk,bhkd->bhqd', p_own, v_own))

    out = lax.map(one_chunk, (jnp.arange(n_chunks), q_chunks))
    return out.transpose(1, 2, 0, 3, 4).reshape(B, H, S, HEAD_DIM)


def hybrid_mixer(h, w_in, b_f, w_o):
    B, S, _ = h.shape
    proj = jnp.einsum('bsd,de->bse', h, w_in)
    cuts = [FOX_WIDTH, 2 * FOX_WIDTH, 3 * FOX_WIDTH, 3 * FOX_WIDTH + N_FOX_HEADS,
            3 * FOX_WIDTH + N_FOX_HEADS + MOBA_WIDTH, 3 * FOX_WIDTH + N_FOX_HEADS + 2 * MOBA_WIDTH]
    fq, fk, fv, f_logit, mq, mk, mv = jnp.split(proj, cuts, axis=-1)
    log_f = jax.nn.log_sigmoid(f_logit.astype(jnp.float32) + b_f.astype(jnp.float32))
    log_f = log_f.transpose(0, 2, 1)
    out_fox = fox_attention(split_heads(fq, N_FOX_HEADS), split_heads(fk, N_FOX_HEADS),
                            split_heads(fv, N_FOX_HEADS), log_f)
    out_moba = moba_attention(split_heads(mq, N_MOBA_HEADS), split_heads(mk, N_MOBA_HEADS),
                              split_heads(mv, N_MOBA_HEADS), alibi_slopes(N_MOBA_HEADS))
    heads = jnp.concatenate([out_fox, out_moba], axis=1)
    heads = heads.transpose(0, 2, 1, 3).reshape(B, S, MIX_WIDTH)
    return jnp.einsum('bse,ed->bsd', heads, w_o)


def swiglu_ffn(h, w_gu, w_down):
    gu = jnp.einsum('bsd,df->bsf', h, w_gu)
    gate, up = jnp.split(gu, [D_FF], axis=-1)
    return jnp.einsum('bsf,fd->bsd', jax.nn.silu(gate) * up, w_down)


def setup_inputs(seed: int = 0) -> dict:
    key = jax.random.key(seed)
    ks = jax.random.split(key, 10)
    x = jax.random.normal(ks[0], (BATCH, SEQ, D_MODEL), jnp.float32)
    w_in = jax.random.normal(ks[1], (DEPTH, D_MODEL, IN_COLS), jnp.float32) * D_MODEL ** -0.5
    col_scale = np.ones((IN_COLS,), np.float32)
    col_scale[2 * FOX_WIDTH:3 * FOX_WIDTH] = BETA
    col_scale[3 * FOX_WIDTH + N_FOX_HEADS + 2 * MOBA_WIDTH:] = BETA
    w_in = w_in * jnp.asarray(col_scale)
    b_f = jnp.linspace(1.0, 5.0, N_FOX_HEADS, dtype=jnp.float32)[None, :] \
        + 0.1 * jax.random.normal(ks[2], (DEPTH, N_FOX_HEADS), jnp.float32)
    w_o = jax.random.normal(ks[3], (DEPTH, MIX_WIDTH, D_MODEL), jnp.float32) * (MIX_WIDTH ** -0.5 * BETA)
    ln1_g = 1.0 + 0.05 * jax.random.normal(ks[4], (DEPTH, D_MODEL), jnp.float32)
    ln1_b = 0.02 * jax.random.normal(ks[5], (DEPTH, D_MODEL), jnp.float32)
    w_gu = jax.random.normal(ks[6], (DEPTH, D_MODEL, 2 * D_FF), jnp.float32) * D_MODEL ** -0.5
    w_down = jax.random.normal(ks[7], (DEPTH, D_FF, D_MODEL), jnp.float32) * (D_FF ** -0.5 * BETA)
    ln2_g = 1.0 + 0.05 * jax.random.normal(ks[8], (DEPTH, D_MODEL), jnp.float32)
    ln2_b = 0.02 * jax.random.normal(ks[9], (DEPTH, D_MODEL), jnp.float32)
    return {"x": x, "w_in": w_in, "b_f": b_f, "w_o": w_o, "ln1_g": ln1_g, "ln1_b": ln1_b,
            "w_gu": w_gu, "w_down": w_down, "ln2_g": ln2_g, "ln2_b": ln2_b}


def reference(x, w_in, b_f, w_o, ln1_g, ln1_b, w_gu, w_down, ln2_g, ln2_b):
    for l in range(DEPTH):
        x = layer_norm(ALPHA * x + hybrid_mixer(x, w_in[l], b_f[l], w_o[l]), ln1_g[l], ln1_b[l])
        x = layer_norm(ALPHA * x + swiglu_ffn(x, w_gu[l], w_down[l]), ln2_g[l], ln2_b[l])
    return x
```

```python
import numpy as np
import ml_dtypes
from contextlib import ExitStack
import concourse.bass as bass
import concourse.mybir as mybir
from concourse.bass_utils import run_bass_kernel_spmd

F32 = mybir.dt.float32
BF16 = mybir.dt.bfloat16
AF = mybir.ActivationFunctionType
ALU = mybir.AluOpType
AX = mybir.AxisListType

PE, ACT, DVE, POOL, SP = "pe", "act", "dve", "pool", "sp"

D = 1024
SEQ = 8192
NB = 4
DEPTH = 2
HD = 64
NH = 16
INC = 3080
DFF = 2816
ALPHA = (2 * DEPTH) ** 0.25
EPS = 1e-5
TQ = 512
NT = SEQ // TQ
NS = NT // 2
R = 108
NCR = R - 64
NEGM = 30000.0
EXP_SHIFT = -20.0


class SemH:
    __slots__ = ("sem", "ndma", "inc")

    def __init__(self, sem, inc):
        self.sem = sem
        self.ndma = 0
        self.inc = inc


class Buf:
    __slots__ = ("name", "base", "writer", "readers", "semh")

    def __init__(self, name, base):
        self.name = name
        self.base = base
        self.writer = None
        self.readers = []
        self.semh = None


class Ins:
    __slots__ = ("eng", "fn", "reads", "writes", "dma_buf", "waits", "marked", "seq", "dma_ord")

    def __init__(self, eng, fn, reads, writes, dma_buf=None):
        self.eng = eng
        self.fn = fn
        self.reads = reads
        self.writes = writes
        self.dma_buf = dma_buf
        self.waits = []
        self.marked = False
        self.seq = 0
        self.dma_ord = 0


class Sched:
    def __init__(self, nc, stack):
        self.nc = nc
        self.stack = stack
        self.ins = []
        self.engs = {PE: nc.tensor, ACT: nc.scalar, DVE: nc.vector, POOL: nc.gpsimd, SP: nc.sync}
        self.esem = {}
        for e in (PE, ACT, DVE, POOL):
            self.esem[e] = stack.enter_context(nc.semaphore("sem_" + e))
        self.nbuf = 0
        self.semhs = {}
        self.pfx = ""

    def buf(self, name=None):
        self.nbuf += 1
        base = name or f"b{self.nbuf}"
        return Buf(self.pfx + base, base)

    def op(self, eng, fn, reads=(), writes=()):
        self.ins.append(Ins(eng, fn, list(reads), list(writes)))

    def dma(self, q, fn, reads=(), writes=(), sem_buf=None, inc=16):
        assert sem_buf is not None
        if sem_buf.semh is None:
            if sem_buf.base not in self.semhs:
                self.semhs[sem_buf.base] = SemH(self.stack.enter_context(self.nc.semaphore("ds_" + sem_buf.base)), inc)
            sem_buf.semh = self.semhs[sem_buf.base]
        self.ins.append(Ins(q, fn, list(reads), list(writes), dma_buf=sem_buf.semh))

    def barrier(self):
        self.ins.append(Ins("barrier", None, [], []))

    def emit(self, final_wait_eng=SP):
        ins = self.ins
        last_on = {}
        for i, I in enumerate(ins):
            if I.eng == "barrier":
                for e, idx in last_on.items():
                    ins[idx].marked = True
                I.waits = (dict(last_on), [(h, h.ndma) for h in self.semhs.values() if h.ndma > 0 and h.inc == 16])
                continue
            if I.dma_buf is None:
                last_on[I.eng] = i
            deps = {}
            for b in I.reads:
                if b.writer is not None:
                    deps[b.writer] = "raw"
            for b in I.writes:
                if b.writer is not None:
                    deps[b.writer] = "waw"
                last = {}
                for r in b.readers:
                    if ins[r].dma_buf is not None:
                        if r not in deps:
                            deps[r] = "war"
                    else:
                        last[ins[r].eng] = r
                for r in last.values():
                    if r not in deps:
                        deps[r] = "war"
            if I.dma_buf is not None:
                I.dma_buf.ndma += 1
                I.dma_ord = I.dma_buf.ndma
            for d, kind in deps.items():
                Dd = ins[d]
                if Dd.dma_buf is not None:
                    h = Dd.dma_buf
                    cnt = h.ndma if h is not I.dma_buf else I.dma_ord - 1
                    I.waits.append((h, h.inc * cnt))
                else:
                    if Dd.eng == I.eng and I.dma_buf is None:
                        if I.eng == PE:
                            continue
                        if kind == "war":
                            continue
                    Dd.marked = True
                    I.waits.append((Dd.eng, d))
            for b in I.reads:
                b.readers.append(i)
            for b in I.writes:
                b.writer = i
                b.readers = []
        cnt = {e: 0 for e in self.esem}
        for I in ins:
            if I.eng != "barrier" and I.dma_buf is None and I.marked:
                cnt[I.eng] += 1
                I.seq = cnt[I.eng]
        waited = {}
        nwait = 0
        for I in ins:
            if I.eng == "barrier":
                lasts, dsn = I.waits
                for E, eng in self.engs.items():
                    for e, idx in lasts.items():
                        wk = (E, ("e", e))
                        val = ins[idx].seq
                        if waited.get(wk, 0) < val:
                            waited[wk] = val
                            eng.wait_ge(self.esem[e], val)
                            nwait += 1
                    for (h, n) in dsn:
                        wk = (E, ("d", id(h)))
                        if waited.get(wk, 0) < h.inc * n:
                            waited[wk] = h.inc * n
                            eng.wait_ge(h.sem, h.inc * n)
                            nwait += 1
                continue
            eng = self.engs[I.eng]
            need = {}
            for w in I.waits:
                if isinstance(w[0], str):
                    key = ("e", w[0])
                    sem = self.esem[w[0]]
                    val = ins[w[1]].seq
                else:
                    key = ("d", id(w[0]))
                    sem = w[0].sem
                    val = w[1]
                if val > need.get(key, (None, 0))[1]:
                    need[key] = (sem, val)
            for key, (sem, val) in need.items():
                wk = (I.eng, key)
                if waited.get(wk, 0) >= val:
                    continue
                waited[wk] = val
                eng.wait_ge(sem, val)
                nwait += 1
            r = I.fn(eng)
            if I.dma_buf is not None:
                r.then_inc(I.dma_buf.sem, I.dma_buf.inc)
            elif I.marked:
                r.then_inc(self.esem[I.eng], 1)
        fe = self.engs[final_wait_eng]
        for h in self.semhs.values():
            fe.wait_ge(h.sem, h.inc * h.ndma)
        for e, c in cnt.items():
            if c > 0:
                fe.wait_ge(self.esem[e], c)
        self.stats = dict(n_ins=len(ins), n_wait=nwait, marked=dict(cnt), n_dma_sems=len(self.semhs))
        self.ins = []
        return self.stats


class Ring:
    def __init__(self, items):
        self.items = items
        self.i = 0

    def next(self):
        it = self.items[self.i % len(self.items)]
        self.i += 1
        return it


def true_tile(p, g):
    j = p // 2
    return 2 * j + g if p % 2 == 0 else 2 * j + 1 - g


_TABLES = {}


def host_tables(g):
    if g in _TABLES:
        return _TABLES[g]
    bf = ml_dtypes.bfloat16
    lt = np.arange(SEQ)
    ttile = np.array([true_tile(p, g) for p in range(NT)])
    tpos = ttile[lt // TQ] * TQ + lt % TQ
    tb = tpos // 256
    rr = tpos % 256
    lb = lt // 256
    own = np.concatenate([np.arange(2 * j * TQ, 2 * j * TQ + TQ) for j in range(NS)])
    kc = np.zeros((NH, NCR, SEQ), np.float32)
    qc = np.zeros((NH, NCR, NS * TQ), np.float32)
    for h in range(8):
        kc[h, 3:6, :] = 1.0
        qc[h, 0:3, :] = 1.0
    for h in range(8, 16):
        slope = 2.0 ** (-(h - 8 + 1))
        kc[h, lb, lt] = 1.0
        kc[h, 32, :] = 1.0
        kc[h, 33, :] = 1.0
        kc[h, 34, :] = slope * 256.0 * tb
        kc[h, 35, :] = slope * rr
        qc[h, 32, :] = -slope * 256.0 * tb[own]
        qc[h, 33, :] = -slope * rr[own]
        qc[h, 34, :] = 1.0
        qc[h, 35, :] = 1.0
    for j in range(NS):
        kc[:, 36 + j, (2 * j + 1) * TQ:(2 * j + 2) * TQ] = 1.0
        qc[:, 36 + j, j * TQ:(j + 1) * TQ] = 0.0 if g == 1 else -NEGM
    masks = np.zeros((128, 8, TQ), np.float32)
    ki = np.arange(128)[:, None]
    qq = np.arange(TQ)[None, :]
    for m in range(4):
        masks[:, m, :] = np.where((m * 128 + ki) <= qq, 0.0, -NEGM)
    for m in range(4, 8):
        masks[:, m, :] = 0.0 if g == 1 else -NEGM
    gbias = np.zeros((128, NS, 4, 32), np.float32)
    ownm = np.zeros((128, NS, 4, 32), np.float32)
    tb_of_lb = np.array([tb[n * 256] for n in range(32)])
    for j in range(NS):
        for s in range(4):
            ob = (2 * j + g) * 2 + s // 2
            gbias[:, j, s, :] = np.where(tb_of_lb < ob, 0.0, -1e30)[None, :]
            ownm[:, j, s, :] = np.where(tb_of_lb >= ob, 1.0, 0.0)[None, :]
    mbig = np.zeros((128, 128), np.float32)
    for p in range(NT):
        for q in range(NT):
            if ttile[q] < ttile[p]:
                for h in range(8):
                    mbig[q * 8 + h, p * 8 + h] = 1.0
    t = dict(kc=kc.astype(bf), qc=qc.astype(bf), masks=masks.astype(bf), gbias=gbias, ownm=ownm,
             mbig=mbig, ident=np.eye(128, dtype=np.float32).astype(bf),
             tri=(np.arange(128)[:, None] <= np.arange(128)[None, :]).astype(np.float32).astype(bf))
    _TABLES[g] = t
    return t


class Ctx:
    pass


def dram_in(nc, name, shape, dt=F32):
    return nc.dram_tensor(name, list(shape), dt, kind="ExternalInput")


def _stage_a(C, l, xg, skip=()):
    nc, S = C.nc, C.S
    with ExitStack() as st:
        def sb(name, shape, dt):
            return st.enter_context(nc.sbuf_tensor(C.pfx + name, list(shape), dt))

        def ps(name, shape, dt=F32):
            return st.enter_context(nc.psum_tensor(C.pfx + name, list(shape), dt))

        win = sb("a_win", [128, 8, INC], BF16)
        b_win = [S.buf(f"win{c}") for c in range(8)]
        if l == 0:
            HW = INC // 2
            wst = Ring([(sb(f"a_wst{i}", [128, HW], F32), S.buf(f"wst{i}")) for i in range(2)])
            for c in range(8):
                for hf in range(2):
                    t, b = wst.next()
                    S.dma(SP, lambda e, t=t, c=c, hf=hf: e.dma_start(
                        out=t[:], in_=C.w_in[l, c * 128:(c + 1) * 128, hf * HW:(hf + 1) * HW]), writes=[b], sem_buf=b)
                    eng = POOL if hf == 0 else DVE
                    S.op(eng, lambda e, t=t, c=c, hf=hf: e.tensor_copy(out=win[:, c, hf * HW:(hf + 1) * HW], in_=t[:]),
                         reads=[b], writes=[b_win[c]])
                for (c0, c1) in ((0, 512), (1544, 2056)):
                    S.op(DVE, lambda e, c=c, c0=c0, c1=c1: e.tensor_scalar(
                        out=win[:, c, c0:c1], in0=win[:, c, c0:c1], scalar1=0.125, scalar2=None, op0=ALU.mult),
                        reads=[b_win[c]], writes=[b_win[c]])
        else:
            for c in range(8):
                S.dma(SP, lambda e, c=c: e.dma_start(out=win[:, c, :], in_=C.WINb[c]),
                      reads=[C.b_WINb], writes=[b_win[c]], sem_buf=b_win[c])
        ident = sb("a_ident", [128, 128], BF16); b_ident = S.buf("ident")
        S.dma(SP, lambda e: e.dma_start(out=ident[:], in_=C.ident.ap()), writes=[b_ident], sem_buf=b_ident)
        gbias = sb("a_gbias", [128, NS, 4, 32], F32); b_gbias = S.buf("gbias")
        S.dma(SP, lambda e: e.dma_start(out=gbias[:], in_=C.gbias.ap()), writes=[b_gbias], sem_buf=b_gbias)
        ownm = sb("a_ownm", [128, NS, 4, 32], F32); b_ownm = S.buf("ownm")
        S.dma(SP, lambda e: e.dma_start(out=ownm[:], in_=C.ownm.ap()), writes=[b_ownm], sem_buf=b_ownm)
        mbig = sb("a_mbig", [128, 128], F32); b_mbig = S.buf("mbig")
        S.dma(SP, lambda e: e.dma_start(out=mbig[:], in_=C.mbig.ap()), writes=[b_mbig], sem_buf=b_mbig)
        negb = sb("a_negb", [128, 1], F32); b_negb = S.buf("negb")
        bsrc = bass.AP(tensor=C.b_f, offset=l * 8, ap=[[0, NT], [1, 8], [1, 1]])
        S.dma(SP, lambda e: e.dma_start(out=negb[:], in_=bsrc), writes=[b_negb], sem_buf=b_negb)
        S.op(DVE, lambda e: e.tensor_scalar(out=negb[:], in0=negb[:], scalar1=-1.0, scalar2=None, op0=ALU.mult),
             reads=[b_negb], writes=[b_negb])
        ones = sb("a_ones", [128, TQ], F32); b_ones = S.buf("ones")
        S.op(POOL, lambda e: e.memset(ones[:], 1.0), writes=[b_ones])
        zf = sb("a_zf", [128, 8, 248], BF16); b_zf = S.buf("zf")
        S.op(POOL, lambda e: e.memset(zf[:], 0.0), writes=[b_zf])
        S.op(POOL, lambda e: e.tensor_copy(out=zf[:, :, 120:128], in_=win[:, :, 1536:1544]), reads=b_win, writes=[b_zf])
        lg_ps = ps("a_lg", [128, TQ], F32); b_lg = S.buf("lg")

        xr = Ring([(sb(f"a_x{i}", [128, 4, D], F32), S.buf(f"ax{i}")) for i in range(2)])
        xr2 = None
        if C.blend:
            xr2 = (sb("a_xsec", [128, 4, D], F32), S.buf("axsec"))
            gsel = sb("a_gsel", [128, 2], F32); b_gsel = S.buf("gsel")
            S.dma(SP, lambda e: e.dma_start(out=gsel[:], in_=C.gsel.ap()), writes=[b_gsel], sem_buf=b_gsel)
        xbr = Ring([(sb(f"a_xb{i}", [128, 4, D], BF16), S.buf(f"axb{i}")) for i in range(2)])
        xT_bufs = [(sb(f"a_xT{i}", [128, 8, TQ], BF16), [S.buf(f"axT{i}_{c}") for c in range(8)]) for i in range(2)]
        tpr = Ring([(ps(f"a_tp{i}", [128, TQ], BF16), S.buf(f"atp{i}")) for i in range(2)])
        bigr = Ring([(ps(f"a_big{i}", [128, TQ], F32), S.buf(f"abig{i}")) for i in range(4)])
        ksb = Ring([(sb(f"a_ksb{i}", [128, TQ], BF16), S.buf(f"aksb{i}")) for i in range(4)])
        qmr = [(sb(f"a_qm{i}", [128, 4, TQ], BF16), [S.buf(f"aqm{i}_{k}") for k in range(4)]) for i in range(2)]
        vpr = Ring([(sb(f"a_vp{i}", [128, 4, 8, 192], BF16), [S.buf(f"avp{i}_{s}") for s in range(4)]) for i in range(2)])
        for (t, bl) in vpr.items:
            S.op(POOL, lambda e, t=t: e.memset(t[:], 1.0), writes=bl)
        kmT = [sb(f"a_kmT{i}", [128, 32], BF16) for i in range(4)]
        b_kmT = [S.buf(f"kmT{i}") for i in range(4)]
        for i in range(4):
            S.op(POOL, lambda e, i=i: e.memset(kmT[i][:], 0.0), writes=[b_kmT[i]])
        kms = Ring([(sb(f"a_kms{i}", [128, 2], F32), S.buf(f"akms{i}")) for i in range(2)])
        gs = Ring([(sb(f"a_gs{i}", [128, 4, 32], F32), S.buf(f"ags{i}")) for i in range(2)])
        t8 = Ring([(sb(f"a_t8{i}", [128, 4, 8], F32), S.buf(f"at8{i}")) for i in range(2)])
        mbr = Ring([(sb(f"a_mb{i}", [128, 4, 32], BF16), S.buf(f"amb{i}")) for i in range(3)])
        mbT_ps = Ring([(ps(f"a_mbTp{i}", [32, TQ], BF16), S.buf(f"ambTp{i}")) for i in range(1)])
        mbT = Ring([(sb(f"a_mbT{i}", [32, TQ], BF16), S.buf(f"ambT{i}")) for i in range(2)])

        evac_i = [0]

        def evac(out_ap, in_ap, reads, writes):
            evac_i[0] += 1
            if evac_i[0] % 2 == 0:
                S.op(ACT, lambda e: e.copy(out=out_ap, in_=in_ap), reads=reads, writes=writes)
            else:
                S.op(DVE, lambda e: e.tensor_copy(out=out_ap, in_=in_ap), reads=reads, writes=writes)

        def proj_fm(xT, b_xT, col0, ncols=128):
            pt, pb = bigr.next()
            for c in range(8):
                S.op(PE, lambda e, c=c, pt=pt: e.matmul(pt[0:ncols, :], lhsT=win[:, c, col0:col0 + ncols],
                                                         rhs=xT[:, c, :], start=(c == 0), stop=(c == 7)),
                     reads=[b_win[c], b_xT[c]], writes=[pb])
            return pt, pb

        def gate1(j, gi, hh):
            qm_t, b_qm = qmr[j % 2]
            pt, pb = bigr.next()
            for s in range(4):
                S.op(PE, lambda e, s=s: e.matmul(
                    pt[:, s * 32:(s + 1) * 32], lhsT=qm_t[hh * 64:(hh + 1) * 64, gi, s * 128:(s + 1) * 128],
                    rhs=kmT[gi][hh * 64:(hh + 1) * 64, :], start=True, stop=True),
                    reads=[b_qm[gi], b_kmT[gi]], writes=[pb])
            gs_t, b_gs = gs.next()
            S.op(DVE, lambda e: e.tensor_tensor(
                out=gs_t[:], in0=pt[:, 0:128].rearrange("p (s n) -> p s n", s=4), in1=gbias[:, j, :, :],
                op=ALU.add), reads=[pb, b_gbias], writes=[b_gs])
            t8_t, b_t8 = t8.next()
            for s in range(4):
                S.op(DVE, lambda e, s=s: e.max(out=t8_t[:, s, :], in_=gs_t[:, s, :]), reads=[b_gs], writes=[b_t8])
            S.op(DVE, lambda e: e.tensor_tensor(
                out=gs_t[:], in0=gs_t[:], in1=t8_t[:, :, 2:3].to_broadcast([128, 4, 32]), op=ALU.is_ge),
                reads=[b_gs, b_t8], writes=[b_gs])
            S.op(DVE, lambda e: e.tensor_tensor(out=gs_t[:], in0=gs_t[:], in1=ownm[:, j, :, :], op=ALU.max),
                 reads=[b_gs, b_ownm], writes=[b_gs])
            mb_t, b_mb = mbr.next()
            S.op(DVE, lambda e: e.tensor_scalar(
                out=mb_t[:], in0=gs_t[:], scalar1=-1.0, scalar2=NEGM, op0=ALU.add, op1=ALU.mult),
                reads=[b_gs], writes=[b_mb])
            return (j, 8 + 2 * gi + hh, mb_t, b_mb)

        def gate2(st_):
            j, h, mb_t, b_mb = st_
            mp_t, b_mp = mbT_ps.next()
            for s in range(4):
                S.op(PE, lambda e, s=s: e.transpose(mp_t[:, s * 128:(s + 1) * 128], mb_t[:, s, :], ident[:]),
                     reads=[b_mb, b_ident], writes=[b_mp])
            mT_t, b_mT = mbT.next()
            evac(mT_t[:], mp_t[:], [b_mp], [b_mT])
            S.dma(SP, lambda e: e.dma_start(out=C.QT[h, 64:96, j * TQ:(j + 1) * TQ], in_=mT_t[:]),
                  reads=[b_mT], writes=[C.b_QT[h]], sem_buf=b_mT)

        gate_pending = []
        xloaded = []

        def xload(p):
            x_t, b_x = xr.next()
            srcs, src_bufs = xg(p)
            S.dma(SP, lambda e, x_t=x_t, a=srcs[0]: e.dma_start(
                out=x_t[:], in_=a.rearrange("(s q) d -> q s d", q=128)), reads=src_bufs, writes=[b_x], sem_buf=b_x)
            x2b = None
            if len(srcs) == 2:
                x2b = xr2
                S.dma(SP, lambda e, a=srcs[1]: e.dma_start(
                    out=xr2[0][:], in_=a.rearrange("(s q) d -> q s d", q=128)), reads=src_bufs, writes=[xr2[1]], sem_buf=xr2[1])
            xloaded.append((x_t, b_x, x2b))

        for j in range(C.npairs):
            own_xT = None
            for half in range(2):
                p = 2 * j + half
                tok0 = p * TQ
                is_own = (half == 0)
                if p == 0:
                    xload(0)
                x_t, b_x, x2b = xloaded.pop(0)
                if p + 1 < 2 * C.npairs:
                    xload(p + 1)
                xb_t, b_xb = xbr.next()
                if x2b is None:
                    S.op(ACT, lambda e, xb_t=xb_t, x_t=x_t: e.copy(out=xb_t[:], in_=x_t[:]),
                         reads=[b_x], writes=[b_xb])
                else:
                    x2_t, b_x2 = x2b
                    S.op(ACT, lambda e, x_t=x_t: e.activation(out=x_t[:], in_=x_t[:], func=AF.Copy, scale=gsel[:, 0:1]),
                         reads=[b_x, b_gsel], writes=[b_x])
                    S.op(DVE, lambda e, xb_t=xb_t, x_t=x_t, x2_t=x2_t: e.scalar_tensor_tensor(
                        out=xb_t[:].rearrange("p s d -> p (s d)"), in0=x2_t[:].rearrange("p s d -> p (s d)"),
                        scalar=gsel[:, 1:2], in1=x_t[:].rearrange("p s d -> p (s d)"), op0=ALU.mult, op1=ALU.add),
                        reads=[b_x, b_x2, b_gsel], writes=[b_xb])
                xT, b_xT = xT_bufs[half]
                for c in range(8):
                    tp, b_tp = tpr.next()
                    for s in range(4):
                        S.op(PE, lambda e, tp=tp, xb_t=xb_t, s=s, c=c: e.transpose(
                            tp[:, s * 128:(s + 1) * 128], xb_t[:, s, c * 128:(c + 1) * 128], ident[:]),
                            reads=[b_xb, b_ident], writes=[b_tp])
                    evac(xT[:, c, :], tp[:], [b_tp], [b_xT[c]])
                g2 = None
                for gi in range(8):
                    if half == 0 and gate_pending:
                        if g2 is not None:
                            gate2(g2)
                        g2 = gate1(*gate_pending.pop(0))
                    col0 = (512 + 128 * gi) if gi < 4 else (2056 + 128 * (gi - 4))
                    h0 = 2 * gi
                    pt, pb = proj_fm(xT, b_xT, col0)
                    kt_t, b_k = ksb.next()
                    evac(kt_t[:], pt[:], [pb], [b_k])
                    for hh in range(0 if 'kdma' in skip else (1 if 'kdma1' in skip else 2)):
                        S.dma(SP, lambda e, kt_t=kt_t, hh=hh, h0=h0, tok0=tok0: e.dma_start(
                            out=C.KT[h0 + hh, 0:64, tok0:tok0 + TQ], in_=kt_t[hh * 64:(hh + 1) * 64, :]),
                            reads=[b_k], writes=[C.b_KT[h0 + hh]], sem_buf=b_k)
                    if gi >= 4 and 'kmean' not in skip:
                        km_t, b_km = kms.next()
                        for bb in range(2):
                            S.op(DVE, lambda e, km_t=km_t, kt_t=kt_t, bb=bb: e.tensor_reduce(
                                out=km_t[:, bb:bb + 1], in_=kt_t[:, bb * 256:(bb + 1) * 256], axis=AX.X, op=ALU.add),
                                reads=[b_k], writes=[b_km])
                        S.op(DVE, lambda e, km_t=km_t, gi=gi, p=p: e.tensor_scalar(
                            out=kmT[gi - 4][:, 2 * p:2 * p + 2], in0=km_t[:], scalar1=1.0 / 256.0, scalar2=None,
                            op0=ALU.mult), reads=[b_km], writes=[b_kmT[gi - 4]])
                if g2 is not None:
                    gate2(g2)
                    g2 = None
                vp_t, b_vp = vpr.next()
                for s in range(4 if 'v' not in skip else 0):
                    for vg in range(2):
                        col0 = 1024 if vg == 0 else 2568
                        pt, pb = bigr.next()
                        for c in range(8):
                            S.op(PE, lambda e, c=c, pt=pt, s=s, col0=col0, xT=xT: e.matmul(
                                pt[:], lhsT=xT[:, c, s * 128:(s + 1) * 128], rhs=win[:, c, col0:col0 + 512],
                                start=(c == 0), stop=(c == 7)), reads=[b_win[c], b_xT[c]], writes=[pb])
                        dst = vp_t[:, s, 4 * vg:4 * vg + 4, :].rearrange("p a (t c) -> p a t c", t=3)[:, :, 0:3:2, :]
                        src = pt[:].rearrange("p (a t c) -> p a t c", a=4, t=2)
                        evac(dst, src, [pb], [b_vp[s]])
                    S.dma(SP, lambda e, vp_t=vp_t, s=s, tok0=tok0: e.dma_start(
                        out=C.VP[:, tok0 + s * 128:tok0 + (s + 1) * 128, :].rearrange("a q c -> q a c"),
                        in_=vp_t[:, s, :, :]), reads=[b_vp[s]], writes=[C.b_VPs[s]], sem_buf=b_vp[s])
                for c in range(8 if 'lg' not in skip else 0):
                    S.op(PE, lambda e, c=c, p=p, xT=xT: e.matmul(
                        lg_ps[:], lhsT=zf[:, c, 120 - 8 * p:248 - 8 * p], rhs=xT[:, c, :],
                        start=(p == 0 and c == 0), stop=(p == NT - 1 and c == 7)),
                        reads=[b_zf, b_xT[c]], writes=[b_lg])
                if is_own and 'q' not in skip:
                    qm_t, b_qm = qmr[j % 2]
                    for gi in range(8):
                        col0 = (128 * gi) if gi < 4 else (1544 + 128 * (gi - 4))
                        h0 = 2 * gi
                        pt, pb = proj_fm(xT, b_xT, col0)
                        if gi < 4:
                            q_t, b_q = ksb.next()
                            evac(q_t[:], pt[:], [pb], [b_q])
                            src_t = q_t
                        else:
                            evac(qm_t[:, gi - 4, :], pt[:], [pb], [b_qm[gi - 4]])
                            b_q = b_qm[gi - 4]
                            src_t = None
                        for hh in range(2):
                            if src_t is not None:
                                in_ap = src_t[hh * 64:(hh + 1) * 64, :]
                            else:
                                in_ap = qm_t[hh * 64:(hh + 1) * 64, gi - 4, :]
                            S.dma(SP, lambda e, in_ap=in_ap, hh=hh, h0=h0, j=j: e.dma_start(
                                out=C.QT[h0 + hh, 0:64, j * TQ:(j + 1) * TQ], in_=in_ap),
                                reads=[b_q], writes=[C.b_QT[h0 + hh]], sem_buf=b_q)
            gate_pending = [(j, gi, hh) for gi in range(4) for hh in range(2)]

        if 'decay' in skip:
            return
        e_t = sb("a_e", [128, TQ], F32); b_e = S.buf("e")
        lf = sb("a_lf", [128, TQ], F32); b_lf = S.buf("lf")
        cs = sb("a_cs", [128, TQ], F32); b_cs = S.buf("cs")
        S.op(ACT, lambda e: e.activation(out=e_t[:], in_=lg_ps[:], func=AF.Exp, bias=negb[:], scale=-1.0),
             reads=[b_lg, b_negb], writes=[b_e])
        S.op(ACT, lambda e: e.activation(out=lf[:], in_=e_t[:], func=AF.Ln, bias=1.0, scale=1.0),
             reads=[b_e], writes=[b_lf])
        S.op(DVE, lambda e: e.tensor_tensor_scan(out=cs[:], data0=ones[:], data1=lf[:], initial=0.0,
                                                 op0=ALU.mult, op1=ALU.add), reads=[b_ones, b_lf], writes=[b_cs])
        tot = sb("a_tot", [128, 2], F32); b_tot = S.buf("tot")
        S.op(DVE, lambda e: e.memset(tot[:], 0.0), writes=[b_tot])
        S.op(DVE, lambda e: e.tensor_copy(out=tot[:, 0:1], in_=cs[:, TQ - 1:TQ]), reads=[b_cs, b_tot], writes=[b_tot])
        pt, pb = bigr.next()
        S.op(PE, lambda e, pt=pt: e.matmul(pt[:, 0:2], lhsT=mbig[:], rhs=tot[:], start=True, stop=True),
             reads=[b_mbig, b_tot], writes=[pb])
        offs = sb("a_offs", [128, 2], F32); b_offs = S.buf("offs")
        S.op(DVE, lambda e, pt=pt: e.tensor_copy(out=offs[:], in_=pt[:, 0:2]), reads=[pb], writes=[b_offs])
        S.op(DVE, lambda e: e.tensor_scalar(out=cs[:], in0=cs[:], scalar1=offs[:, 0:1], scalar2=None, op0=ALU.add),
             reads=[b_cs, b_offs], writes=[b_cs])
        hs = [sb(f"a_h{i}", [128, TQ], BF16) for i in range(3)]
        b_hs = [S.buf(f"h{i}") for i in range(3)]
        nhs = [sb(f"a_nh{i}", [128, TQ], BF16) for i in range(3)]
        b_nhs = [S.buf(f"nh{i}") for i in range(3)]
        for i in range(3):
            S.op(DVE, lambda e, i=i: e.tensor_copy(out=hs[i][:], in_=cs[:]), reads=[b_cs], writes=[b_hs[i]])
            if i < 2:
                S.op(DVE, lambda e, i=i: e.tensor_tensor(out=cs[:], in0=cs[:], in1=hs[i][:], op=ALU.subtract),
                     reads=[b_cs, b_hs[i]], writes=[b_cs])
            S.op(DVE, lambda e, i=i: e.tensor_scalar(out=nhs[i][:], in0=hs[i][:], scalar1=-1.0, scalar2=None,
                                                     op0=ALU.mult), reads=[b_hs[i]], writes=[b_nhs[i]])
            for p in range(NT):
                S.dma(SP, lambda e, i=i, p=p: e.dma_start(out=C.KT[0:8, 64 + i, p * TQ:(p + 1) * TQ],
                                                          in_=hs[i][8 * p:8 * p + 8, :]),
                      reads=[b_hs[i]], writes=[C.b_KTc[p]], sem_buf=b_hs[i])
                if p % 2 == 0:
                    j = p // 2
                    S.dma(SP, lambda e, i=i, p=p, j=j: e.dma_start(out=C.QT[0:8, 67 + i, j * TQ:(j + 1) * TQ],
                                                                   in_=nhs[i][8 * p:8 * p + 8, :]),
                          reads=[b_nhs[i]], writes=[C.b_QTc[j]], sem_buf=b_nhs[i])

        g2 = None
        for u in gate_pending:
            if g2 is not None:
                gate2(g2)
            g2 = gate1(*u)
        if g2 is not None:
            gate2(g2)


def _stage_b(C, l):
    nc, S = C.nc, C.S
    FL = ''
    with ExitStack() as st:
        def sb(name, shape, dt):
            return st.enter_context(nc.sbuf_tensor(C.pfx + name, list(shape), dt))

        def ps(name, shape, dt=F32):
            return st.enter_context(nc.psum_tensor(C.pfx + name, list(shape), dt))

        tri = sb("b_tri", [128, 128], BF16); b_tri = S.buf("tri")
        S.dma(SP, lambda e: e.dma_start(out=tri[:], in_=C.tri.ap()), writes=[b_tri], sem_buf=b_tri)
        onesf = sb("b_onesf", [128, 128], BF16); b_onesf = S.buf("onesf")
        S.op(POOL, lambda e: e.memset(onesf[:], 1.0), writes=[b_onesf])
        ktr = [[(sb(f"b_kt{i}_{hd}", [R, SEQ], BF16), [S.buf(f"bkt{i}_{hd}a"), S.buf(f"bkt{i}_{hd}b")]) for hd in range(2)]
               for i in range(2)]
        qtr = [[(sb(f"b_qt{i}_{hd}", [R, NS * TQ], BF16), S.buf(f"bqt{i}_{hd}")) for hd in range(2)] for i in range(2)]
        vvr = [(sb(f"b_vv{i}", [128, SEQ // 128, 192], BF16), [S.buf(f"bvv{i}_{q4}") for q4 in range(4)]) for i in range(2)]
        sps = Ring([(ps(f"b_s{i}", [128, 2 * TQ]), S.buf(f"bs{i}")) for i in range(2)])
        accr = Ring([(ps(f"b_acc{i}", [128, TQ]), S.buf(f"bacc{i}")) for i in range(3)])
        bcp = Ring([(ps("b_bcp", [128, TQ]), S.buf("bbcp"))])
        pr = Ring([(sb(f"b_p{i}", [128, 2 * TQ], BF16), S.buf(f"bp{i}")) for i in range(4)])
        rcr = Ring([(sb(f"b_rc{i}", [128, TQ], F32), S.buf(f"brc{i}")) for i in range(3)])
        r12r = Ring([(sb(f"b_r12{i}", [128, 2, TQ], BF16), S.buf(f"br12{i}")) for i in range(3)])
        bcr = Ring([(sb(f"b_bc{i}", [128, TQ], F32), S.buf(f"bbc{i}")) for i in range(2)])
        osr = Ring([(sb(f"b_o{i}", [128, TQ], BF16), S.buf(f"bo{i}")) for i in range(2)])

        def loads(i):
            for hd in range(2):
                h = 2 * i + hd
                kt_t, b_kt = ktr[i % 2][hd]
                S.dma(SP, lambda e, kt_t=kt_t, h=h: e.dma_start(out=kt_t[:, 0:1024], in_=C.KT[h, :, 0:1024]),
                      reads=[C.b_KT[h]] + (C.b_KTc if h < 8 else []), writes=[b_kt[0]], sem_buf=b_kt[0])
                S.dma(SP, lambda e, kt_t=kt_t, h=h: e.dma_start(out=kt_t[:, 1024:SEQ], in_=C.KT[h, :, 1024:SEQ]),
                      reads=[C.b_KT[h]] + (C.b_KTc if h < 8 else []), writes=[b_kt[1]], sem_buf=b_kt[1])
                qt_t, b_qt = qtr[i % 2][hd]
                S.dma(SP, lambda e, qt_t=qt_t, h=h: e.dma_start(out=qt_t[:], in_=C.QT[h]),
                      reads=[C.b_QT[h]] + (C.b_QTc if h < 8 else []), writes=[b_qt], sem_buf=b_qt)
            vv_t, b_vv = vvr[i % 2]
            for q4 in range(4):
                S.dma(SP, lambda e, vv_t=vv_t, i=i, q4=q4: e.dma_start(
                    out=vv_t[:, q4 * 16:(q4 + 1) * 16, :],
                    in_=C.VP[i, q4 * 2048:(q4 + 1) * 2048, :].rearrange("(k q) c -> q k c", q=128)),
                    reads=[C.b_VP[i]] + C.b_VPs, writes=[b_vv[q4]], sem_buf=b_vv[q4])

        its = []
        for i in range(C.nheadpairs):
            for j in range(C.npairs):
                for hd in range(2):
                    n2 = 4 * j + 4
                    for k2 in range(n2):
                        its.append((i, j, hd, k2, n2))
        state = {}

        def unit(i, j, hd):
            key = (i, j, hd)
            if key not in state:
                if hd == 0:
                    state[("o", i, j)] = osr.next()
                state[key] = accr.next()
            return state[key]

        def col0(it, t):
            i, j, hd, k2, n2 = it
            if n2 - 4 <= k2 < n2 - 2 and 'nocol' not in FL:
                return 128 * (2 * (k2 - (n2 - 4)) + t)
            return 0

        def s_mm(it):
            i, j, hd, k2, n2 = it
            kt_t, b_kt = ktr[i % 2][hd]
            qt_t, b_qt = qtr[i % 2][hd]
            s_t, b_s = sps.next()
            for t in range(2):
                kt = 2 * k2 + t
                c0 = col0(it, t)
                S.op(PE, lambda e, s_t=s_t, kt=kt, t=t, kt_t=kt_t, qt_t=qt_t, j=j, c0=c0: e.matmul(
                    s_t[:, t * TQ + c0:(t + 1) * TQ], lhsT=kt_t[:, kt * 128:(kt + 1) * 128],
                    rhs=qt_t[:, j * TQ + c0:(j + 1) * TQ], start=True, stop=True),
                    reads=[b_kt[0 if kt < 8 else 1], b_qt], writes=[b_s])
            return s_t, b_s

        def tail_a(i, j, hd):
            acc, b_acc = unit(i, j, hd)
            prow = 64 if hd == 0 else 0
            rc_t, b_rc = rcr.next()
            r12_t, b_r12 = r12r.next()
            pr_ = slice(prow, prow + 1)
            S.op(DVE, lambda e: e.tensor_copy(out=r12_t[pr_, 0, :], in_=acc[pr_, :]), reads=[b_acc], writes=[b_r12])
            S.op(DVE, lambda e: e.tensor_tensor(out=rc_t[pr_, :], in0=acc[pr_, :], in1=r12_t[pr_, 0, :], op=ALU.subtract),
                 reads=[b_acc, b_r12], writes=[b_rc])
            S.op(DVE, lambda e: e.tensor_copy(out=r12_t[pr_, 1, :], in_=rc_t[pr_, :]), reads=[b_rc, b_r12], writes=[b_r12])
            return r12_t, b_r12

        def tail_b(i, j, hd, rc):
            r12_t, b_r12 = rc
            acc, b_acc = unit(i, j, hd)
            o_t, b_o = state[("o", i, j)]
            prow = 64 if hd == 0 else 0
            o0 = 0 if hd == 0 else 64
            bp_t, b_bp = bcp.next()
            for k in range(2):
                S.op(PE, lambda e, k=k: e.matmul(
                    bp_t[:], lhsT=onesf[prow:prow + 1, :], rhs=r12_t[prow:prow + 1, k, :], start=(k == 0), stop=(k == 1)),
                    reads=[b_onesf, b_r12], writes=[b_bp])
            bc_t, b_bc = bcr.next()
            S.op(DVE, lambda e: e.reciprocal(out=bc_t[o0:o0 + 64, :], in_=bp_t[o0:o0 + 64, :]), reads=[b_bp], writes=[b_bc])
            S.op(DVE, lambda e: e.tensor_tensor(
                out=o_t[o0:o0 + 64, :], in0=acc[o0:o0 + 64, :], in1=bc_t[o0:o0 + 64, :], op=ALU.mult),
                reads=[b_acc, b_bc], writes=[b_o])
            if hd == 1:
                S.dma(SP, lambda e: e.dma_start(
                    out=C.OT[i, :, j * TQ:(j + 1) * TQ], in_=o_t[:]), reads=[b_o], writes=[C.b_OT[i]], sem_buf=b_o)

        loads(0)
        if C.nheadpairs > 1:
            loads(1)
        QW = DFF // 2
        wsr = Ring([(sb(f"b_ws{i}", [128, QW], F32), S.buf(f"bws{i}")) for i in range(2)])
        wbr = Ring([(sb(f"b_wb{i}", [128, QW], BF16), S.buf(f"bwb{i}")) for i in range(2)])
        precast = []

        def add_chunk(src, dst, n, b_dst, qscale=None, dst_is_3d=False):
            st_ = {}

            def load():
                st_["ws"] = wsr.next()
                ws_t, b_ws = st_["ws"]
                S.dma(SP, lambda e: e.dma_start(out=ws_t[:, 0:n], in_=src), writes=[b_ws], sem_buf=b_ws)

            def cast():
                ws_t, b_ws = st_["ws"]
                st_["wb"] = wbr.next()
                wb_t, b_wb = st_["wb"]
                S.op(POOL, lambda e: e.tensor_copy(out=wb_t[:, 0:n], in_=ws_t[:, 0:n]), reads=[b_ws], writes=[b_wb])
                if qscale is not None:
                    q0, q1 = qscale
                    S.op(POOL, lambda e: e.tensor_scalar(out=wb_t[:, q0:q1], in0=wb_t[:, q0:q1], scalar1=0.125, scalar2=None,
                                                         op0=ALU.mult), reads=[b_wb], writes=[b_wb])

            def store():
                wb_t, b_wb = st_["wb"]
                src_ap = wb_t[:, 0:n].rearrange("p (f k) -> p f k", k=128) if dst_is_3d else wb_t[:, 0:n]
                S.dma(SP, lambda e: e.dma_start(out=dst, in_=src_ap), reads=[b_wb], writes=[b_dst], sem_buf=b_wb)
            precast.append((load, cast, store))

        nfq = QW // 128
        for c in range(8):
            for hf in range(2):
                for qq in range(2):
                    col = hf * DFF + qq * QW
                    add_chunk(C.w_gu[l, c * 128:(c + 1) * 128, col:col + QW],
                              C.WGU[qq * nfq:(qq + 1) * nfq, :, c, hf * 128:(hf + 1) * 128].rearrange("f p k -> p f k"),
                              QW, C.b_WGU, dst_is_3d=True)
        for i8 in range(8):
            add_chunk(C.w_o[l, i8 * 128:(i8 + 1) * 128, :], C.WOb[i8], D, C.b_WOb)
        for f in range(DFF // 128):
            add_chunk(C.w_down[l, f * 128:(f + 1) * 128, :], C.WDb[f], D, C.b_WDb)
        if l + 1 < DEPTH:
            for c in range(8):
                for (c0, c1, qs) in ((0, 1408, (0, 512)), (1408, 2816, (136, 648)), (2816, INC, None)):
                    add_chunk(C.w_in[l + 1, c * 128:(c + 1) * 128, c0:c1], C.WINb[c, :, c0:c1], c1 - c0, C.b_WINb, qs)
        pc_state = {"k": 0}

        def precast_step():
            k = pc_state["k"]
            pc_state["k"] += 1
            if k - 2 >= 0 and k - 2 < len(precast):
                precast[k - 2][2]()
            if k - 1 >= 0 and k - 1 < len(precast):
                precast[k - 1][1]()
            if k < len(precast):
                precast[k][0]()
            return k - 2 >= len(precast) - 1
        q = [s_mm(its[0])]
        if len(its) > 1:
            q.append(s_mm(its[1]))
        pending = []
        for n, it in enumerate(its):
            i, j, hd, k2, n2 = it
            if k2 == 0 and j == 0 and hd == 0 and i >= 1 and i + 1 < C.nheadpairs:
                loads(i + 1)
            s_t, b_s = q.pop(0)
            while pending and pending[0][0] <= n:
                tail_b(*pending.pop(0)[1])
            if n % 16 == 5 and n >= 133 and pc_state["k"] < len(precast) + 2:
                precast_step()
            acc, b_acc = unit(i, j, hd)
            vv_t, b_vv = vvr[i % 2]
            voff = 0 if hd == 0 else 64
            p_t, b_p = pr.next()
            if n2 - 4 <= k2 < n2 - 2:
                ca = col0(it, 0)
                S.op(ACT, lambda e, p_t=p_t, s_t=s_t, ca=ca: e.activation(
                    out=p_t[:].rearrange("p (t c) -> p t c", t=2)[:, :, ca:TQ],
                    in_=s_t[:].rearrange("p (t c) -> p t c", t=2)[:, :, ca:TQ], func=AF.Exp, bias=EXP_SHIFT),
                    reads=[b_s], writes=[b_p])
            else:
                S.op(ACT, lambda e, p_t=p_t, s_t=s_t: e.activation(out=p_t[:], in_=s_t[:], func=AF.Exp, bias=EXP_SHIFT),
                     reads=[b_s], writes=[b_p])
            if n + 2 < len(its):
                q.append(s_mm(its[n + 2]))
            for t in range(2):
                c0 = col0(it, t)
                if n2 - 4 <= k2 < n2 - 2 and 'notri' not in FL:
                    S.op(POOL, lambda e, p_t=p_t, t=t, c0=c0: e.tensor_tensor(
                        out=p_t[:, t * TQ + c0:t * TQ + c0 + 128], in0=p_t[:, t * TQ + c0:t * TQ + c0 + 128],
                        in1=tri[:], op=ALU.mult), reads=[b_p, b_tri], writes=[b_p])
            for t in range(2):
                kt = 2 * k2 + t
                c0 = col0(it, t)
                S.op(PE, lambda e, acc=acc, vv_t=vv_t, kt=kt, t=t, voff=voff, p_t=p_t, k2=k2, n2=n2, c0=c0: e.matmul(
                    acc[:, c0:TQ], lhsT=vv_t[:, kt, voff:voff + 128], rhs=p_t[:, t * TQ + c0:(t + 1) * TQ],
                    start=(k2 == 0 and t == 0), stop=(k2 == n2 - 1 and t == 1)), reads=[b_vv[kt // 16], b_p], writes=[b_acc])
            if k2 == n2 - 1:
                rc = tail_a(i, j, hd)
                pending.append((n + 5, (i, j, hd, rc)))
        while pending:
            tail_b(*pending.pop(0)[1])
        while pc_state["k"] < len(precast) + 2:
            precast_step()


def stage_a(C, l, xg, skip=()):
    _stage_a(C, l, xg, skip)
    C.S.barrier()


def stage_b(C, l):
    _stage_b(C, l)
    C.S.barrier()


def stage_c(C, l, xg, xout, after_slot=None):
    _stage_c(C, l, xg, xout, after_slot)
    C.S.barrier()


def _stage_c(C, l, xg, xout, after_slot=None):
    nc, S = C.nc, C.S
    NF = DFF // 128
    with ExitStack() as st:
        def sb(name, shape, dt):
            return st.enter_context(nc.sbuf_tensor(C.pfx + name, list(shape), dt))

        def ps(name, shape, dt=F32):
            return st.enter_context(nc.psum_tensor(C.pfx + name, list(shape), dt))

        ident = sb("c_ident", [128, 128], BF16); b_ident = S.buf("cident")
        S.dma(SP, lambda e: e.dma_start(out=ident[:], in_=C.ident.ap()), writes=[b_ident], sem_buf=b_ident)
        wo = sb("c_wo", [128, 8, D], BF16); b_wo = S.buf("wo")
        wd = sb("c_wd", [128, NF, D], BF16); b_wd = [S.buf(f"wd{f}") for f in range(NF)]
        S.dma(SP, lambda e: e.dma_start(out=wo[:], in_=C.WOb.ap().rearrange("i p d -> p i d")),
              reads=[C.b_WOb], writes=[b_wo], sem_buf=b_wo)
        lnt = []
        for nm, src in (("g1", C.ln1_g), ("b1", C.ln1_b), ("g2", C.ln2_g), ("b2", C.ln2_b)):
            t = sb("c_ln" + nm, [128, D], F32); b = S.buf("ln" + nm)
            ap = bass.AP(tensor=src, offset=l * D, ap=[[0, 128], [1, D]])
            S.dma(SP, lambda e, t=t, ap=ap: e.dma_start(out=t[:], in_=ap), writes=[b], sem_buf=b)
            lnt.append((t, b))
        (g1, b_g1), (b1, b_b1), (g2, b_g2), (b2, b_b2) = lnt

        otr = Ring([(sb(f"c_ot{i}", [128, 8, TQ], BF16), S.buf(f"cot{i}")) for i in range(2)])
        xr = Ring([(sb(f"c_x{i}", [128, D], F32), S.buf(f"cx{i}")) for i in range(2)])
        x1s = [sb(f"c_xone{k}", [128, 4, D], F32) for k in range(2)]
        b_x1s = [[S.buf(f"cx1_{k}_{s}") for s in range(4)] for k in range(2)]
        rr_ = Ring([(sb(f"c_r{i}", [128, D], F32), S.buf(f"cr{i}")) for i in range(2)])
        x1br = Ring([(sb(f"c_x1b{i}", [128, D], BF16), S.buf(f"cx1b{i}")) for i in range(2)])
        x1Ts = [sb(f"c_x1T{k}", [128, 8, TQ], BF16) for k in range(2)]
        b_x1Ts = [[S.buf(f"cx1T{k}_{s}") for s in range(4)] for k in range(2)]
        hT = sb("c_hT", [128, NF, TQ], BF16); b_hT = [S.buf(f"chT{f}") for f in range(NF)]
        wgr = Ring([(sb(f"c_wg{i}", [128, 8, 256], BF16), S.buf(f"cwg{i}")) for i in range(3)])
        sgr = Ring([(sb(f"c_sg{i}", [128, TQ], F32), S.buf(f"csg{i}")) for i in range(2)])
        yr = Ring([(sb(f"c_y{i}", [128, D], F32), S.buf(f"cy{i}")) for i in range(2)])
        str_ = Ring([(sb(f"c_st{i}", [128, 2, 6], F32), S.buf(f"cst{i}")) for i in range(2)])
        mvr = Ring([(sb(f"c_mv{i}", [128, 8], F32), S.buf(f"cmv{i}")) for i in range(2)])
        bigr = Ring([(ps(f"c_big{i}", [128, TQ]), S.buf(f"cbig{i}")) for i in range(6)])
        tpr = Ring([(ps(f"c_tp{i}", [128, TQ], BF16), S.buf(f"ctp{i}")) for i in range(2)])

        epst = sb("c_eps", [128, 1], F32); b_epst = S.buf("ceps")
        S.op(POOL, lambda e: e.memset(epst[:], EPS), writes=[b_epst])

        def layer_norm(src_t, b_src, dst_ap, b_dst, gt, b_gt, bt, b_bt):
            st_t, b_st = str_.next()
            for k in range(2):
                S.op(DVE, lambda e, st_t=st_t, k=k: e.bn_stats(out=st_t[:, k, :], in_=src_t[:, k * 512:(k + 1) * 512]),
                     reads=[b_src], writes=[b_st])
            mv_t, b_mv = mvr.next()
            S.op(DVE, lambda e, mv_t=mv_t, st_t=st_t: e.bn_aggr(out=mv_t[:, 0:2], in_=st_t[:].rearrange("p a b -> p (a b)")),
                 reads=[b_st], writes=[b_mv])
            S.op(DVE, lambda e, mv_t=mv_t: e.tensor_scalar(out=mv_t[:, 3:4], in0=mv_t[:, 1:2], scalar1=EPS, scalar2=-0.5,
                                                           op0=ALU.add, op1=ALU.mult), reads=[b_mv], writes=[b_mv])
            S.op(DVE, lambda e, mv_t=mv_t: e.tensor_scalar(out=mv_t[:, 4:5], in0=mv_t[:, 3:4], scalar1=-1.0, scalar2=0.5,
                                                           op0=ALU.mult, op1=ALU.add), reads=[b_mv], writes=[b_mv])
            S.op(DVE, lambda e, mv_t=mv_t: e.reciprocal(out=mv_t[:, 2:3], in_=mv_t[:, 4:5]), reads=[b_mv], writes=[b_mv])
            for it_ in range(7):
                S.op(DVE, lambda e, mv_t=mv_t: e.scalar_tensor_tensor(
                    out=mv_t[:, 5:6], in0=mv_t[:, 2:3], scalar=mv_t[:, 2:3], in1=mv_t[:, 3:4], op0=ALU.mult, op1=ALU.mult),
                    reads=[b_mv], writes=[b_mv])
                S.op(DVE, lambda e, mv_t=mv_t: e.scalar_tensor_tensor(
                    out=mv_t[:, 2:3], in0=mv_t[:, 5:6], scalar=1.5, in1=mv_t[:, 2:3], op0=ALU.add, op1=ALU.mult),
                    reads=[b_mv], writes=[b_mv])
            S.op(DVE, lambda e, mv_t=mv_t: e.tensor_scalar(out=src_t[:], in0=src_t[:], scalar1=mv_t[:, 0:1],
                                                           scalar2=mv_t[:, 2:3], op0=ALU.subtract, op1=ALU.mult),
                 reads=[b_src, b_mv], writes=[b_src])
            S.op(POOL, lambda e: e.tensor_tensor(out=src_t[:], in0=src_t[:], in1=gt[:], op=ALU.mult),
                 reads=[b_src, b_gt], writes=[b_src])
            S.op(POOL, lambda e: e.tensor_tensor(out=dst_ap, in0=src_t[:], in1=bt[:], op=ALU.add),
                 reads=[b_src, b_bt], writes=[b_dst])

        ots = {}

        def p1a(j, s):
            x1, b_x1 = x1s[j % 2], b_x1s[j % 2]
            if s == 0:
                ot_t, b_ot = otr.next()
                S.dma(SP, lambda e: e.dma_start(
                    out=ot_t[:], in_=C.OT[:, :, j * TQ:(j + 1) * TQ].rearrange("i p t -> p i t")),
                    reads=C.b_OT, writes=[b_ot], sem_buf=b_ot)
                ots[j] = (ot_t, b_ot)
            ot_t, b_ot = ots[j]
            x_t, b_x = xr.next()
            xa, xa_bufs = xg(j, s)
            S.dma(SP, lambda e: e.dma_start(out=x_t[:], in_=xa), reads=xa_bufs, writes=[b_x], sem_buf=b_x)
            r_t, b_r = rr_.next()
            for hf in range(2):
                pt, pb = bigr.next()
                for i in range(8):
                    S.op(PE, lambda e, pt=pt, i=i, hf=hf: e.matmul(
                        pt[:], lhsT=ot_t[:, i, s * 128:(s + 1) * 128], rhs=wo[:, i, hf * 512:(hf + 1) * 512],
                        start=(i == 0), stop=(i == 7)), reads=[b_ot, b_wo], writes=[pb])
                S.op(DVE, lambda e, pt=pt, hf=hf: e.scalar_tensor_tensor(
                    out=r_t[:, hf * 512:(hf + 1) * 512], in0=x_t[:, hf * 512:(hf + 1) * 512], scalar=ALPHA,
                    in1=pt[:], op0=ALU.mult, op1=ALU.add), reads=[b_x, pb], writes=[b_r])
            layer_norm(r_t, b_r, x1[:, s, :], b_x1[s], g1, b_g1, b1, b_b1)

        def p1b(j, s):
            x1, b_x1 = x1s[j % 2], b_x1s[j % 2]
            x1T, b_x1T = x1Ts[j % 2], b_x1Ts[j % 2]
            xb_t, b_xb = x1br.next()
            S.op(ACT, lambda e: e.copy(out=xb_t[:], in_=x1[:, s, :]), reads=[b_x1[s]], writes=[b_xb])
            for cg in range(2):
                tp, b_tp = tpr.next()
                for k in range(4):
                    cc = 4 * cg + k
                    S.op(PE, lambda e, tp=tp, k=k, cc=cc: e.transpose(
                        tp[:, k * 128:(k + 1) * 128], xb_t[:, cc * 128:(cc + 1) * 128], ident[:]),
                        reads=[b_xb, b_ident], writes=[b_tp])
                S.op(ACT, lambda e, tp=tp, cg=cg: e.copy(
                    out=x1T[:, 4 * cg:4 * cg + 4, s * 128:(s + 1) * 128],
                    in_=tp[:].rearrange("p (k t) -> p k t", k=4)), reads=[b_tp], writes=[b_x1T[s]])

        wg_issued = [0]
        wg_q = []
        ystore = []
        for s in range(4):
            p1a(0, s)
        for s in range(4):
            p1b(0, s)
        for k in range(2):
            S.dma(SP, lambda e, k=k: e.dma_start(out=wd[:, 11 * k:11 * k + 11, :],
                                                 in_=C.WDb[11 * k:11 * k + 11].rearrange("f p d -> p f d")),
                  reads=[C.b_WDb], writes=b_wd[11 * k:11 * k + 11], sem_buf=b_wd[11 * k])
        for j in range(C.npairs):
            x1, b_x1 = x1s[j % 2], b_x1s[j % 2]
            x1T, b_x1T = x1Ts[j % 2], b_x1Ts[j % 2]
            nxt = j + 1 < C.npairs
            for f in range(NF):
                if f == 4 and ystore:
                    ystore.pop(0)()
                if nxt and f % 5 == 1 and f // 5 < 4:
                    p1a(j + 1, f // 5)
                if nxt and f >= 6 and (f - 6) % 5 == 0 and (f - 6) // 5 < 3:
                    p1b(j + 1, (f - 6) // 5)
                g_idx = j * NF + f
                while wg_issued[0] < min(g_idx + 3, C.npairs * NF):
                    gi_ = wg_issued[0]
                    wt, wb_ = wgr.next()
                    S.dma(SP, lambda e, wt=wt, ff=gi_ % NF: e.dma_start(out=wt[:], in_=C.WGU[ff]),
                          reads=[C.b_WGU], writes=[wb_], sem_buf=wb_)
                    wg_q.append((wt, wb_))
                    wg_issued[0] += 1
                wg_t, b_wg = wg_q.pop(0)
                gp, b_gp = bigr.next()
                up, b_up = bigr.next()
                for (pp, bb, o) in ((gp, b_gp, 0), (up, b_up, 128)):
                    for cc in range(8):
                        S.op(PE, lambda e, pp=pp, wg_t=wg_t, cc=cc, o=o, x1T=x1T: e.matmul(
                            pp[:], lhsT=wg_t[:, cc, o:o + 128], rhs=x1T[:, cc, :], start=(cc == 0), stop=(cc == 7)),
                            reads=[b_wg] + b_x1T, writes=[bb])
                sg_t, b_sg = sgr.next()
                S.op(ACT, lambda e, sg_t=sg_t, gp=gp: e.activation(out=sg_t[:], in_=gp[:], func=AF.Silu),
                     reads=[b_gp], writes=[b_sg])
                S.op(DVE, lambda e, sg_t=sg_t, up=up, f=f: e.tensor_tensor(
                    out=hT[:, f, :], in0=sg_t[:], in1=up[:], op=ALU.mult), reads=[b_sg, b_up], writes=[b_hT[f]])
            for s in range(4):
                if nxt and s == 1:
                    p1b(j + 1, 3)
                r_t, b_r = rr_.next()
                for hf in range(2):
                    pt, pb = bigr.next()
                    for f in range(NF):
                        S.op(PE, lambda e, pt=pt, f=f, s=s, hf=hf: e.matmul(
                            pt[:], lhsT=hT[:, f, s * 128:(s + 1) * 128], rhs=wd[:, f, hf * 512:(hf + 1) * 512],
                            start=(f == 0), stop=(f == NF - 1)), reads=[b_hT[f], b_wd[f]], writes=[pb])
                    S.op(DVE, lambda e, r_t=r_t, pt=pt, hf=hf, s=s, x1=x1: e.scalar_tensor_tensor(
                        out=r_t[:, hf * 512:(hf + 1) * 512], in0=x1[:, s, hf * 512:(hf + 1) * 512], scalar=ALPHA,
                        in1=pt[:], op0=ALU.mult, op1=ALU.add), reads=[b_x1[s], pb], writes=[b_r])
                if ystore:
                    ystore.pop(0)()
                y_t, b_y = yr.next()
                layer_norm(r_t, b_r, y_t[:], b_y, g2, b_g2, b2, b_b2)

                def st_fn(y_t=y_t, b_y=b_y, j=j, s=s):
                    S.dma(SP, lambda e: e.dma_start(out=xout[0][j, s * 128:(s + 1) * 128, :], in_=y_t[:]),
                          reads=[b_y], writes=xout[1](j), sem_buf=b_y)
                    if s == 3 and after_slot is not None:
                        after_slot(j)
                ystore.append(st_fn)
        while ystore:
            ystore.pop(0)()


def build_program(debug=False, skip=()):
    import os
    nc = bass.Bass("TRN2", target_bir_lowering=False)
    C = Ctx()
    C.nc = nc
    C.dbg = None
    C.npairs = NS
    C.nheadpairs = 8
    C.xg = dram_in(nc, "xg", [NT, TQ, D])
    C.w_in = dram_in(nc, "w_in", [DEPTH, D, INC])
    C.b_f = dram_in(nc, "b_f", [DEPTH, 8])
    C.w_o = dram_in(nc, "w_o", [DEPTH, D, D])
    C.w_gu = dram_in(nc, "w_gu", [DEPTH, D, 2 * DFF])
    C.w_down = dram_in(nc, "w_down", [DEPTH, DFF, D])
    C.ln1_g = dram_in(nc, "ln1_g", [DEPTH, D])
    C.ln1_b = dram_in(nc, "ln1_b", [DEPTH, D])
    C.ln2_g = dram_in(nc, "ln2_g", [DEPTH, D])
    C.ln2_b = dram_in(nc, "ln2_b", [DEPTH, D])
    C.kc = dram_in(nc, "kc", [NH, NCR, SEQ], BF16)
    C.qc = dram_in(nc, "qc", [NH, NCR, NS * TQ], BF16)
    C.masks = dram_in(nc, "masks", [128, 8, TQ], BF16)
    C.gbias = dram_in(nc, "gbias", [128, NS, 4, 32])
    C.ownm = dram_in(nc, "ownm", [128, NS, 4, 32])
    C.mbig = dram_in(nc, "mbig", [128, 128])
    C.ident = dram_in(nc, "ident", [128, 128], BF16)
    C.gsel = dram_in(nc, "gsel", [128, 2])
    C.tri = dram_in(nc, "tri", [128, 128], BF16)
    C.KT = nc.dram_tensor("KT", [NH, R, SEQ], BF16)
    C.QT = nc.dram_tensor("QT", [NH, R, NS * TQ], BF16)
    C.VP = nc.dram_tensor("VP", [8, SEQ, 192], BF16)
    C.OT = nc.dram_tensor("OT", [8, 128, NS * TQ], BF16)
    C.WGU = nc.dram_tensor("WGU", [DFF // 128, 128, 8, 256], BF16)
    C.WOb = nc.dram_tensor("WOb", [8, 128, D], BF16)
    C.WDb = nc.dram_tensor("WDb", [DFF // 128, 128, D], BF16)
    C.WINb = nc.dram_tensor("WINb", [8, 128, INC], BF16)
    C.x2own = nc.dram_tensor("x2own", [NS, TQ, D], F32)
    C.x2all = nc.dram_tensor("x2all", [NS, 2 * TQ, D], F32)
    C.xout = nc.dram_tensor("xout", [NS, TQ, D], F32, kind="ExternalOutput")
    with ExitStack() as st:
        S = Sched(nc, st)
        C.S = S
        C.b_KT = [S.buf(f"KT{h}") for h in range(NH)]
        C.b_QT = [S.buf(f"QT{h}") for h in range(NH)]
        C.b_VP = [S.buf(f"VP{i}") for i in range(8)]
        C.b_VPs = [S.buf(f"VPs{i}") for i in range(4)]
        C.b_KTc = [S.buf(f"KTc{p}") for p in range(NT)]
        C.b_QTc = [S.buf(f"QTc{j}") for j in range(NS)]
        C.b_OT = [S.buf(f"OT{i}") for i in range(8)]
        C.b_WGU = S.buf("WGU")
        C.b_WOb = S.buf("WOb")
        C.b_WDb = S.buf("WDb")
        C.b_WINb = S.buf("WINb")
        b_x2own = [S.buf(f"x2own{j}") for j in range(NS)]
        b_x2all = [S.buf(f"x2all{j}") for j in range(NS)]
        b_cc = S.buf("constcopy")
        for h in range(NH):
            S.dma(SP, lambda e, h=h: e.dma_start(out=C.KT[h, 64:R, :], in_=C.kc[h]), writes=[C.b_KT[h]], sem_buf=b_cc)
            S.dma(SP, lambda e, h=h: e.dma_start(out=C.QT[h, 64:R, :], in_=C.qc[h]), writes=[C.b_QT[h]], sem_buf=b_cc)
        C.pfx = S.pfx = "L0_"
        C.blend = False
        stage_a(C, 0, lambda p: ([C.xg[p]], []), skip=skip)
        stage_b(C, 0)
        b_coll = S.buf("coll")

        def exchange(j):
            S.dma(POOL, lambda e, j=j: e.collective_compute(
                "AllGather", ALU.bypass, replica_groups=[[0, 1], [2, 3], [4, 5], [6, 7]],
                ins=[C.x2own[j]], outs=[C.x2all[j]]),
                reads=[b_x2own[j]], writes=[b_x2all[j]], sem_buf=b_coll, inc=1)
        stage_c(C, 0, lambda j, s: (C.xg[2 * j, s * 128:(s + 1) * 128, :], []), (C.x2own, lambda j: [b_x2own[j]]),
                after_slot=exchange)
        C.pfx = S.pfx = "L1_"
        C.blend = True

        def xsrc1(p):
            j = p // 2
            if p % 2 == 0:
                return [C.x2own[j]], [b_x2own[j]]
            return [C.x2all[j, 0:TQ, :], C.x2all[j, TQ:2 * TQ, :]], [b_x2all[j]]
        stage_a(C, 1, xsrc1, skip=skip)
        stage_b(C, 1)
        stage_c(C, 1, lambda j, s: (C.x2own[j, s * 128:(s + 1) * 128, :], [b_x2own[j]]), (C.xout, lambda j: []))
        print(S.emit())
    return nc


_NC = {}


def kernel(x, w_in, b_f, w_o, ln1_g, ln1_b, w_gu, w_down, ln2_g, ln2_b):
    x = np.asarray(x, np.float32)
    if "nc" not in _NC:
        _NC["nc"] = build_program()
    nc = _NC["nc"]
    f32c = lambda a: np.ascontiguousarray(np.asarray(a, np.float32))
    shared = dict(w_in=f32c(w_in), b_f=f32c(b_f), w_o=f32c(w_o), w_gu=f32c(w_gu), w_down=f32c(w_down),
                  ln1_g=f32c(ln1_g), ln1_b=f32c(ln1_b), ln2_g=f32c(ln2_g), ln2_b=f32c(ln2_b))
    in_maps = []
    for c in range(8):
        b, g = c // 2, c % 2
        t = host_tables(g)
        xg = np.ascontiguousarray(x[b].reshape(NT, TQ, D)[[true_tile(p, g) for p in range(NT)]])
        gsel = np.zeros((128, 2), np.float32)
        gsel[:, 0] = float(g)
        gsel[:, 1] = float(1 - g)
        in_maps.append(dict(xg=xg, gsel=gsel, **shared, **t))
    res = run_bass_kernel_spmd(nc, in_maps, core_ids=list(range(8)))
    out = np.empty_like(x)
    for c in range(8):
        b, g = c // 2, c % 2
        xo = np.asarray(res.results[c]["xout"], np.float32)
        for j in range(NS):
            T = 2 * j + g
            out[b, T * TQ:(T + 1) * TQ] = xo[j]
    return out
```

```python
import numpy as np
import ml_dtypes
from contextlib import ExitStack
import concourse.bass as bass
import concourse.mybir as mybir
from concourse.bass_utils import run_bass_kernel_spmd

F32 = mybir.dt.float32
BF16 = mybir.dt.bfloat16
AF = mybir.ActivationFunctionType
ALU = mybir.AluOpType
AX = mybir.AxisListType

PE, ACT, DVE, POOL, SP = "pe", "act", "dve", "pool", "sp"

D = 1024
SEQ = 8192
NB = 4
DEPTH = 2
HD = 64
NH = 16
INC = 3080
DFF = 2816
ALPHA = (2 * DEPTH) ** 0.25
EPS = 1e-5
TQ = 512
NT = SEQ // TQ
NS = NT // 2
R = 108
NCR = R - 64
NEGM = 30000.0
EXP_SHIFT = -20.0


class SemH:
    __slots__ = ("sem", "ndma", "inc")

    def __init__(self, sem, inc):
        self.sem = sem
        self.ndma = 0
        self.inc = inc


class Buf:
    __slots__ = ("name", "base", "writer", "readers", "semh")

    def __init__(self, name, base):
        self.name = name
        self.base = base
        self.writer = None
        self.readers = []
        self.semh = None


class Ins:
    __slots__ = ("eng", "fn", "reads", "writes", "dma_buf", "waits", "marked", "seq", "dma_ord")

    def __init__(self, eng, fn, reads, writes, dma_buf=None):
        self.eng = eng
        self.fn = fn
        self.reads = reads
        self.writes = writes
        self.dma_buf = dma_buf
        self.waits = []
        self.marked = False
        self.seq = 0
        self.dma_ord = 0


class Sched:
    def __init__(self, nc, stack):
        self.nc = nc
        self.stack = stack
        self.ins = []
        self.engs = {PE: nc.tensor, ACT: nc.scalar, DVE: nc.vector, POOL: nc.gpsimd, SP: nc.sync}
        self.esem = {}
        for e in (PE, ACT, DVE, POOL):
            self.esem[e] = stack.enter_context(nc.semaphore("sem_" + e))
        self.nbuf = 0
        self.semhs = {}
        self.pfx = ""

    def buf(self, name=None):
        self.nbuf += 1
        base = name or f"b{self.nbuf}"
        return Buf(self.pfx + base, base)

    def op(self, eng, fn, reads=(), writes=()):
        self.ins.append(Ins(eng, fn, list(reads), list(writes)))

    def dma(self, q, fn, reads=(), writes=(), sem_buf=None, inc=16):
        assert sem_buf is not None
        if sem_buf.semh is None:
            if sem_buf.base not in self.semhs:
                self.semhs[sem_buf.base] = SemH(self.stack.enter_context(self.nc.semaphore("ds_" + sem_buf.base)), inc)
            sem_buf.semh = self.semhs[sem_buf.base]
        self.ins.append(Ins(q, fn, list(reads), list(writes), dma_buf=sem_buf.semh))

    def barrier(self):
        self.ins.append(Ins("barrier", None, [], []))

    def emit(self, final_wait_eng=SP):
        ins = self.ins
        last_on = {}
        for i, I in enumerate(ins):
            if I.eng == "barrier":
                for e, idx in last_on.items():
                    ins[idx].marked = True
                I.waits = (dict(last_on), [(h, h.ndma) for h in self.semhs.values() if h.ndma > 0 and h.inc == 16])
                continue
            if I.dma_buf is None:
                last_on[I.eng] = i
            deps = {}
            for b in I.reads:
                if b.writer is not None:
                    deps[b.writer] = "raw"
            for b in I.writes:
                if b.writer is not None:
                    deps[b.writer] = "waw"
                last = {}
                for r in b.readers:
                    if ins[r].dma_buf is not None:
                        if r not in deps:
                            deps[r] = "war"
                    else:
                        last[ins[r].eng] = r
                for r in last.values():
                    if r not in deps:
                        deps[r] = "war"
            if I.dma_buf is not None:
                I.dma_buf.ndma += 1
                I.dma_ord = I.dma_buf.ndma
            for d, kind in deps.items():
                Dd = ins[d]
                if Dd.dma_buf is not None:
                    h = Dd.dma_buf
                    cnt = h.ndma if h is not I.dma_buf else I.dma_ord - 1
                    I.waits.append((h, h.inc * cnt))
                else:
                    if Dd.eng == I.eng and I.dma_buf is None:
                        if I.eng == PE:
                            continue
                        if kind == "war":
                            continue
                    Dd.marked = True
                    I.waits.append((Dd.eng, d))
            for b in I.reads:
                b.readers.append(i)
            for b in I.writes:
                b.writer = i
                b.readers = []
        cnt = {e: 0 for e in self.esem}
        for I in ins:
            if I.eng != "barrier" and I.dma_buf is None and I.marked:
                cnt[I.eng] += 1
                I.seq = cnt[I.eng]
        waited = {}
        nwait = 0
        for I in ins:
            if I.eng == "barrier":
                lasts, dsn = I.waits
                for E, eng in self.engs.items():
                    for e, idx in lasts.items():
                        wk = (E, ("e", e))
                        val = ins[idx].seq
                        if waited.get(wk, 0) < val:
                            waited[wk] = val
                            eng.wait_ge(self.esem[e], val)
                            nwait += 1
                    for (h, n) in dsn:
                        wk = (E, ("d", id(h)))
                        if waited.get(wk, 0) < h.inc * n:
                            waited[wk] = h.inc * n
                            eng.wait_ge(h.sem, h.inc * n)
                            nwait += 1
                continue
            eng = self.engs[I.eng]
            need = {}
            for w in I.waits:
                if isinstance(w[0], str):
                    key = ("e", w[0])
                    sem = self.esem[w[0]]
                    val = ins[w[1]].seq
                else:
                    key = ("d", id(w[0]))
                    sem = w[0].sem
                    val = w[1]
                if val > need.get(key, (None, 0))[1]:
                    need[key] = (sem, val)
            for key, (sem, val) in need.items():
                wk = (I.eng, key)
                if waited.get(wk, 0) >= val:
                    continue
                waited[wk] = val
                eng.wait_ge(sem, val)
                nwait += 1
            r = I.fn(eng)
            if I.dma_buf is not None:
                r.then_inc(I.dma_buf.sem, I.dma_buf.inc)
            elif I.marked:
                r.then_inc(self.esem[I.eng], 1)
        fe = self.engs[final_wait_eng]
        for h in self.semhs.values():
            fe.wait_ge(h.sem, h.inc * h.ndma)
        for e, c in cnt.items():
            if c > 0:
                fe.wait_ge(self.esem[e], c)
        self.stats = dict(n_ins=len(ins), n_wait=nwait, marked=dict(cnt), n_dma_sems=len(self.semhs))
        self.ins = []
        return self.stats


class Ring:
    def __init__(self, items):
        self.items = items
        self.i = 0

    def next(self):
        it = self.items[self.i % len(self.items)]
        self.i += 1
        return it


def true_tile(p, g):
    j = p // 2
    return 2 * j + g if p % 2 == 0 else 2 * j + 1 - g


_TABLES = {}


def host_tables(g):
    if g in _TABLES:
        return _TABLES[g]
    bf = ml_dtypes.bfloat16
    lt = np.arange(SEQ)
    ttile = np.array([true_tile(p, g) for p in range(NT)])
    tpos = ttile[lt // TQ] * TQ + lt % TQ
    tb = tpos // 256
    rr = tpos % 256
    lb = lt // 256
    own = np.concatenate([np.arange(2 * j * TQ, 2 * j * TQ + TQ) for j in range(NS)])
    kc = np.zeros((NH, NCR, SEQ), np.float32)
    qc = np.zeros((NH, NCR, NS * TQ), np.float32)
    for h in range(8):
        kc[h, 3:6, :] = 1.0
        qc[h, 0:3, :] = 1.0
    for h in range(8, 16):
        slope = 2.0 ** (-(h - 8 + 1))
        kc[h, lb, lt] = 1.0
        kc[h, 32, :] = 1.0
        kc[h, 33, :] = 1.0
        kc[h, 34, :] = slope * 256.0 * tb
        kc[h, 35, :] = slope * rr
        qc[h, 32, :] = -slope * 256.0 * tb[own]
        qc[h, 33, :] = -slope * rr[own]
        qc[h, 34, :] = 1.0
        qc[h, 35, :] = 1.0
    for j in range(NS):
        kc[:, 36 + j, (2 * j + 1) * TQ:(2 * j + 2) * TQ] = 1.0
        qc[:, 36 + j, j * TQ:(j + 1) * TQ] = 0.0 if g == 1 else -NEGM
    masks = np.zeros((128, 8, TQ), np.float32)
    ki = np.arange(128)[:, None]
    qq = np.arange(TQ)[None, :]
    for m in range(4):
        masks[:, m, :] = np.where((m * 128 + ki) <= qq, 0.0, -NEGM)
    for m in range(4, 8):
        masks[:, m, :] = 0.0 if g == 1 else -NEGM
    gbias = np.zeros((128, NS, 4, 32), np.float32)
    ownm = np.zeros((128, NS, 4, 32), np.float32)
    tb_of_lb = np.array([tb[n * 256] for n in range(32)])
    for j in range(NS):
        for s in range(4):
            ob = (2 * j + g) * 2 + s // 2
            gbias[:, j, s, :] = np.where(tb_of_lb < ob, 0.0, -1e30)[None, :]
            ownm[:, j, s, :] = np.where(tb_of_lb >= ob, 1.0, 0.0)[None, :]
    mbig = np.zeros((128, 128), np.float32)
    for p in range(NT):
        for q in range(NT):
            if ttile[q] < ttile[p]:
                for h in range(8):
                    mbig[q * 8 + h, p * 8 + h] = 1.0
    t = dict(kc=kc.astype(bf), qc=qc.astype(bf), masks=masks.astype(bf), gbias=gbias, ownm=ownm,
             mbig=mbig, ident=np.eye(128, dtype=np.float32).astype(bf),
             tri=(np.arange(128)[:, None] <= np.arange(128)[None, :]).astype(np.float32).astype(bf))
    _TABLES[g] = t
    return t


class Ctx:
    pass


def dram_in(nc, name, shape, dt=F32):
    return nc.dram_tensor(name, list(shape), dt, kind="ExternalInput")


def _stage_a(C, l, xg, skip=()):
    nc, S = C.nc, C.S
    with ExitStack() as st:
        def sb(name, shape, dt):
            return st.enter_context(nc.sbuf_tensor(C.pfx + name, list(shape), dt))

        def ps(name, shape, dt=F32):
            return st.enter_context(nc.psum_tensor(C.pfx + name, list(shape), dt))

        win = sb("a_win", [128, 8, INC], BF16)
        b_win = [S.buf(f"win{c}") for c in range(8)]
        if l == 0:
            HW = INC // 2
            wst = Ring([(sb(f"a_wst{i}", [128, HW], F32), S.buf(f"wst{i}")) for i in range(2)])
            for c in range(8):
                for hf in range(2):
                    t, b = wst.next()
                    S.dma(SP, lambda e, t=t, c=c, hf=hf: e.dma_start(
                        out=t[:], in_=C.w_in[l, c * 128:(c + 1) * 128, hf * HW:(hf + 1) * HW]), writes=[b], sem_buf=b)
                    eng = POOL if hf == 0 else DVE
                    S.op(eng, lambda e, t=t, c=c, hf=hf: e.tensor_copy(out=win[:, c, hf * HW:(hf + 1) * HW], in_=t[:]),
                         reads=[b], writes=[b_win[c]])
                for (c0, c1) in ((0, 512), (1544, 2056)):
                    S.op(DVE, lambda e, c=c, c0=c0, c1=c1: e.tensor_scalar(
                        out=win[:, c, c0:c1], in0=win[:, c, c0:c1], scalar1=0.125, scalar2=None, op0=ALU.mult),
                        reads=[b_win[c]], writes=[b_win[c]])
        else:
            for c in range(8):
                S.dma(SP, lambda e, c=c: e.dma_start(out=win[:, c, :], in_=C.WINb[c]),
                      reads=[C.b_WINb], writes=[b_win[c]], sem_buf=b_win[c])
        ident = sb("a_ident", [128, 128], BF16); b_ident = S.buf("ident")
        S.dma(SP, lambda e: e.dma_start(out=ident[:], in_=C.ident.ap()), writes=[b_ident], sem_buf=b_ident)
        gbias = sb("a_gbias", [128, NS, 4, 32], F32); b_gbias = S.buf("gbias")
        S.dma(SP, lambda e: e.dma_start(out=gbias[:], in_=C.gbias.ap()), writes=[b_gbias], sem_buf=b_gbias)
        ownm = sb("a_ownm", [128, NS, 4, 32], F32); b_ownm = S.buf("ownm")
        S.dma(SP, lambda e: e.dma_start(out=ownm[:], in_=C.ownm.ap()), writes=[b_ownm], sem_buf=b_ownm)
        mbig = sb("a_mbig", [128, 128], F32); b_mbig = S.buf("mbig")
        S.dma(SP, lambda e: e.dma_start(out=mbig[:], in_=C.mbig.ap()), writes=[b_mbig], sem_buf=b_mbig)
        negb = sb("a_negb", [128, 1], F32); b_negb = S.buf("negb")
        bsrc = bass.AP(tensor=C.b_f, offset=l * 8, ap=[[0, NT], [1, 8], [1, 1]])
        S.dma(SP, lambda e: e.dma_start(out=negb[:], in_=bsrc), writes=[b_negb], sem_buf=b_negb)
        S.op(DVE, lambda e: e.tensor_scalar(out=negb[:], in0=negb[:], scalar1=-1.0, scalar2=None, op0=ALU.mult),
             reads=[b_negb], writes=[b_negb])
        ones = sb("a_ones", [128, TQ], F32); b_ones = S.buf("ones")
        S.op(POOL, lambda e: e.memset(ones[:], 1.0), writes=[b_ones])
        zf = sb("a_zf", [128, 8, 248], BF16); b_zf = S.buf("zf")
        S.op(POOL, lambda e: e.memset(zf[:], 0.0), writes=[b_zf])
        S.op(POOL, lambda e: e.tensor_copy(out=zf[:, :, 120:128], in_=win[:, :, 1536:1544]), reads=b_win, writes=[b_zf])
        lg_ps = ps("a_lg", [128, TQ], F32); b_lg = S.buf("lg")

        xr = Ring([(sb(f"a_x{i}", [128, 4, D], F32), S.buf(f"ax{i}")) for i in range(2)])
        xr2 = None
        if C.blend:
            xr2 = (sb("a_xsec", [128, 4, D], F32), S.buf("axsec"))
            gsel = sb("a_gsel", [128, 2], F32); b_gsel = S.buf("gsel")
            S.dma(SP, lambda e: e.dma_start(out=gsel[:], in_=C.gsel.ap()), writes=[b_gsel], sem_buf=b_gsel)
        xbr = Ring([(sb(f"a_xb{i}", [128, 4, D], BF16), S.buf(f"axb{i}")) for i in range(2)])
        xT_bufs = [(sb(f"a_xT{i}", [128, 8, TQ], BF16), [S.buf(f"axT{i}_{c}") for c in range(8)]) for i in range(2)]
        tpr = Ring([(ps(f"a_tp{i}", [128, TQ], BF16), S.buf(f"atp{i}")) for i in range(2)])
        bigr = Ring([(ps(f"a_big{i}", [128, TQ], F32), S.buf(f"abig{i}")) for i in range(4)])
        ksb = Ring([(sb(f"a_ksb{i}", [128, TQ], BF16), S.buf(f"aksb{i}")) for i in range(4)])
        qmr = [(sb(f"a_qm{i}", [128, 4, TQ], BF16), [S.buf(f"aqm{i}_{k}") for k in range(4)]) for i in range(2)]
        vpr = Ring([(sb(f"a_vp{i}", [128, 4, 8, 192], BF16), [S.buf(f"avp{i}_{s}") for s in range(4)]) for i in range(2)])
        for (t, bl) in vpr.items:
            S.op(POOL, lambda e, t=t: e.memset(t[:], 1.0), writes=bl)
        kmT = [sb(f"a_kmT{i}", [128, 32], BF16) for i in range(4)]
        b_kmT = [S.buf(f"kmT{i}") for i in range(4)]
        for i in range(4):
            S.op(POOL, lambda e, i=i: e.memset(kmT[i][:], 0.0), writes=[b_kmT[i]])
        kms = Ring([(sb(f"a_kms{i}", [128, 2], F32), S.buf(f"akms{i}")) for i in range(2)])
        gs = Ring([(sb(f"a_gs{i}", [128, 4, 32], F32), S.buf(f"ags{i}")) for i in range(2)])
        t8 = Ring([(sb(f"a_t8{i}", [128, 4, 8], F32), S.buf(f"at8{i}")) for i in range(2)])
        mbr = Ring([(sb(f"a_mb{i}", [128, 4, 32], BF16), S.buf(f"amb{i}")) for i in range(3)])
        mbT_ps = Ring([(ps(f"a_mbTp{i}", [32, TQ], BF16), S.buf(f"ambTp{i}")) for i in range(1)])
        mbT = Ring([(sb(f"a_mbT{i}", [32, TQ], BF16), S.buf(f"ambT{i}")) for i in range(2)])

        evac_i = [0]

        def evac(out_ap, in_ap, reads, writes):
            evac_i[0] += 1
            if evac_i[0] % 3 != 0:
                S.op(ACT, lambda e: e.copy(out=out_ap, in_=in_ap), reads=reads, writes=writes)
            else:
                S.op(DVE, lambda e: e.tensor_copy(out=out_ap, in_=in_ap), reads=reads, writes=writes)

        def proj_fm(xT, b_xT, col0, ncols=128):
            pt, pb = bigr.next()
            for c in range(8):
                S.op(PE, lambda e, c=c, pt=pt: e.matmul(pt[0:ncols, :], lhsT=win[:, c, col0:col0 + ncols],
                                                         rhs=xT[:, c, :], start=(c == 0), stop=(c == 7)),
                     reads=[b_win[c], b_xT[c]], writes=[pb])
            return pt, pb

        def gate1(j, gi, hh):
            qm_t, b_qm = qmr[j % 2]
            pt, pb = bigr.next()
            for s in range(4):
                S.op(PE, lambda e, s=s: e.matmul(
                    pt[:, s * 32:(s + 1) * 32], lhsT=qm_t[hh * 64:(hh + 1) * 64, gi, s * 128:(s + 1) * 128],
                    rhs=kmT[gi][hh * 64:(hh + 1) * 64, :], start=True, stop=True),
                    reads=[b_qm[gi], b_kmT[gi]], writes=[pb])
            gs_t, b_gs = gs.next()
            S.op(DVE, lambda e: e.tensor_tensor(
                out=gs_t[:], in0=pt[:, 0:128].rearrange("p (s n) -> p s n", s=4), in1=gbias[:, j, :, :],
                op=ALU.add), reads=[pb, b_gbias], writes=[b_gs])
            t8_t, b_t8 = t8.next()
            for s in range(4):
                S.op(DVE, lambda e, s=s: e.max(out=t8_t[:, s, :], in_=gs_t[:, s, :]), reads=[b_gs], writes=[b_t8])
            S.op(DVE, lambda e: e.tensor_tensor(
                out=gs_t[:], in0=gs_t[:], in1=t8_t[:, :, 2:3].to_broadcast([128, 4, 32]), op=ALU.is_ge),
                reads=[b_gs, b_t8], writes=[b_gs])
            S.op(DVE, lambda e: e.tensor_tensor(out=gs_t[:], in0=gs_t[:], in1=ownm[:, j, :, :], op=ALU.max),
                 reads=[b_gs, b_ownm], writes=[b_gs])
            mb_t, b_mb = mbr.next()
            S.op(DVE, lambda e: e.tensor_scalar(
                out=mb_t[:], in0=gs_t[:], scalar1=-1.0, scalar2=NEGM, op0=ALU.add, op1=ALU.mult),
                reads=[b_gs], writes=[b_mb])
            return (j, 8 + 2 * gi + hh, mb_t, b_mb)

        def gate2(st_):
            j, h, mb_t, b_mb = st_
            mp_t, b_mp = mbT_ps.next()
            for s in range(4):
                S.op(PE, lambda e, s=s: e.transpose(mp_t[:, s * 128:(s + 1) * 128], mb_t[:, s, :], ident[:]),
                     reads=[b_mb, b_ident], writes=[b_mp])
            mT_t, b_mT = mbT.next()
            evac(mT_t[:], mp_t[:], [b_mp], [b_mT])
            S.dma(SP, lambda e: e.dma_start(out=C.QT[h, 64:96, j * TQ:(j + 1) * TQ], in_=mT_t[:]),
                  reads=[b_mT], writes=[C.b_QT[h]], sem_buf=b_mT)

        gate_pending = []
        xloaded = []

        def xload(p):
            x_t, b_x = xr.next()
            srcs, src_bufs = xg(p)
            S.dma(SP, lambda e, x_t=x_t, a=srcs[0]: e.dma_start(
                out=x_t[:], in_=a.rearrange("(s q) d -> q s d", q=128)), reads=src_bufs, writes=[b_x], sem_buf=b_x)
            x2b = None
            if len(srcs) == 2:
                x2b = xr2
                S.dma(SP, lambda e, a=srcs[1]: e.dma_start(
                    out=xr2[0][:], in_=a.rearrange("(s q) d -> q s d", q=128)), reads=src_bufs, writes=[xr2[1]], sem_buf=xr2[1])
            xloaded.append((x_t, b_x, x2b))

        for j in range(C.npairs):
            own_xT = None
            for half in range(2):
                p = 2 * j + half
                tok0 = p * TQ
                is_own = (half == 0)
                if p == 0:
                    xload(0)
                x_t, b_x, x2b = xloaded.pop(0)
                if p + 1 < 2 * C.npairs:
                    xload(p + 1)
                xb_t, b_xb = xbr.next()
                if x2b is None:
                    S.op(ACT, lambda e, xb_t=xb_t, x_t=x_t: e.copy(out=xb_t[:], in_=x_t[:]),
                         reads=[b_x], writes=[b_xb])
                else:
                    x2_t, b_x2 = x2b
                    S.op(ACT, lambda e, x_t=x_t: e.activation(out=x_t[:], in_=x_t[:], func=AF.Copy, scale=gsel[:, 0:1]),
                         reads=[b_x, b_gsel], writes=[b_x])
                    S.op(DVE, lambda e, xb_t=xb_t, x_t=x_t, x2_t=x2_t: e.scalar_tensor_tensor(
                        out=xb_t[:].rearrange("p s d -> p (s d)"), in0=x2_t[:].rearrange("p s d -> p (s d)"),
                        scalar=gsel[:, 1:2], in1=x_t[:].rearrange("p s d -> p (s d)"), op0=ALU.mult, op1=ALU.add),
                        reads=[b_x, b_x2, b_gsel], writes=[b_xb])
                xT, b_xT = xT_bufs[half]
                for c in range(8):
                    tp, b_tp = tpr.next()
                    for s in range(4):
                        S.op(PE, lambda e, tp=tp, xb_t=xb_t, s=s, c=c: e.transpose(
                            tp[:, s * 128:(s + 1) * 128], xb_t[:, s, c * 128:(c + 1) * 128], ident[:]),
                            reads=[b_xb, b_ident], writes=[b_tp])
                    evac(xT[:, c, :], tp[:], [b_tp], [b_xT[c]])
                g2 = None
                for gi in range(8):
                    if half == 0 and gate_pending:
                        if g2 is not None:
                            gate2(g2)
                        g2 = gate1(*gate_pending.pop(0))
                    col0 = (512 + 128 * gi) if gi < 4 else (2056 + 128 * (gi - 4))
                    h0 = 2 * gi
                    pt, pb = proj_fm(xT, b_xT, col0)
                    kt_t, b_k = ksb.next()
                    evac(kt_t[:], pt[:], [pb], [b_k])
                    for hh in range(0 if 'kdma' in skip else (1 if 'kdma1' in skip else 2)):
                        S.dma(SP, lambda e, kt_t=kt_t, hh=hh, h0=h0, tok0=tok0: e.dma_start(
                            out=C.KT[h0 + hh, 0:64, tok0:tok0 + TQ], in_=kt_t[hh * 64:(hh + 1) * 64, :]),
                            reads=[b_k], writes=[C.b_KT[h0 + hh]], sem_buf=b_k)
                    if gi >= 4 and 'kmean' not in skip:
                        km_t, b_km = kms.next()
                        for bb in range(2):
                            S.op(DVE, lambda e, km_t=km_t, kt_t=kt_t, bb=bb: e.tensor_reduce(
                                out=km_t[:, bb:bb + 1], in_=kt_t[:, bb * 256:(bb + 1) * 256], axis=AX.X, op=ALU.add),
                                reads=[b_k], writes=[b_km])
                        S.op(DVE, lambda e, km_t=km_t, gi=gi, p=p: e.tensor_scalar(
                            out=kmT[gi - 4][:, 2 * p:2 * p + 2], in0=km_t[:], scalar1=1.0 / 256.0, scalar2=None,
                            op0=ALU.mult), reads=[b_km], writes=[b_kmT[gi - 4]])
                if g2 is not None:
                    gate2(g2)
                    g2 = None
                vp_t, b_vp = vpr.next()
                for s in range(4 if 'v' not in skip else 0):
                    for vg in range(2):
                        col0 = 1024 if vg == 0 else 2568
                        pt, pb = bigr.next()
                        for c in range(8):
                            S.op(PE, lambda e, c=c, pt=pt, s=s, col0=col0, xT=xT: e.matmul(
                                pt[:], lhsT=xT[:, c, s * 128:(s + 1) * 128], rhs=win[:, c, col0:col0 + 512],
                                start=(c == 0), stop=(c == 7)), reads=[b_win[c], b_xT[c]], writes=[pb])
                        dst = vp_t[:, s, 4 * vg:4 * vg + 4, :].rearrange("p a (t c) -> p a t c", t=3)[:, :, 0:3:2, :]
                        src = pt[:].rearrange("p (a t c) -> p a t c", a=4, t=2)
                        evac(dst, src, [pb], [b_vp[s]])
                    S.dma(SP, lambda e, vp_t=vp_t, s=s, tok0=tok0: e.dma_start(
                        out=C.VP[:, tok0 + s * 128:tok0 + (s + 1) * 128, :].rearrange("a q c -> q a c"),
                        in_=vp_t[:, s, :, :]), reads=[b_vp[s]], writes=[C.b_VPs[s]], sem_buf=b_vp[s])
                for c in range(8 if 'lg' not in skip else 0):
                    S.op(PE, lambda e, c=c, p=p, xT=xT: e.matmul(
                        lg_ps[:], lhsT=zf[:, c, 120 - 8 * p:248 - 8 * p], rhs=xT[:, c, :],
                        start=(p == 0 and c == 0), stop=(p == NT - 1 and c == 7)),
                        reads=[b_zf, b_xT[c]], writes=[b_lg])
                if is_own and 'q' not in skip:
                    qm_t, b_qm = qmr[j % 2]
                    for gi in range(8):
                        col0 = (128 * gi) if gi < 4 else (1544 + 128 * (gi - 4))
                        h0 = 2 * gi
                        pt, pb = proj_fm(xT, b_xT, col0)
                        if gi < 4:
                            q_t, b_q = ksb.next()
                            evac(q_t[:], pt[:], [pb], [b_q])
                            src_t = q_t
                        else:
                            evac(qm_t[:, gi - 4, :], pt[:], [pb], [b_qm[gi - 4]])
                            b_q = b_qm[gi - 4]
                            src_t = None
                        for hh in range(2):
                            if src_t is not None:
                                in_ap = src_t[hh * 64:(hh + 1) * 64, :]
                            else:
                                in_ap = qm_t[hh * 64:(hh + 1) * 64, gi - 4, :]
                            S.dma(SP, lambda e, in_ap=in_ap, hh=hh, h0=h0, j=j: e.dma_start(
                                out=C.QT[h0 + hh, 0:64, j * TQ:(j + 1) * TQ], in_=in_ap),
                                reads=[b_q], writes=[C.b_QT[h0 + hh]], sem_buf=b_q)
            gate_pending = [(j, gi, hh) for gi in range(4) for hh in range(2)]

        if 'decay' in skip:
            return
        e_t = sb("a_e", [128, TQ], F32); b_e = S.buf("e")
        lf = sb("a_lf", [128, TQ], F32); b_lf = S.buf("lf")
        cs = sb("a_cs", [128, TQ], F32); b_cs = S.buf("cs")
        S.op(ACT, lambda e: e.activation(out=e_t[:], in_=lg_ps[:], func=AF.Exp, bias=negb[:], scale=-1.0),
             reads=[b_lg, b_negb], writes=[b_e])
        S.op(ACT, lambda e: e.activation(out=lf[:], in_=e_t[:], func=AF.Ln, bias=1.0, scale=1.0),
             reads=[b_e], writes=[b_lf])
        S.op(DVE, lambda e: e.tensor_tensor_scan(out=cs[:], data0=ones[:], data1=lf[:], initial=0.0,
                                                 op0=ALU.mult, op1=ALU.add), reads=[b_ones, b_lf], writes=[b_cs])
        tot = sb("a_tot", [128, 2], F32); b_tot = S.buf("tot")
        S.op(DVE, lambda e: e.memset(tot[:], 0.0), writes=[b_tot])
        S.op(DVE, lambda e: e.tensor_copy(out=tot[:, 0:1], in_=cs[:, TQ - 1:TQ]), reads=[b_cs, b_tot], writes=[b_tot])
        pt, pb = bigr.next()
        S.op(PE, lambda e, pt=pt: e.matmul(pt[:, 0:2], lhsT=mbig[:], rhs=tot[:], start=True, stop=True),
             reads=[b_mbig, b_tot], writes=[pb])
        offs = sb("a_offs", [128, 2], F32); b_offs = S.buf("offs")
        S.op(DVE, lambda e, pt=pt: e.tensor_copy(out=offs[:], in_=pt[:, 0:2]), reads=[pb], writes=[b_offs])
        S.op(DVE, lambda e: e.tensor_scalar(out=cs[:], in0=cs[:], scalar1=offs[:, 0:1], scalar2=None, op0=ALU.add),
             reads=[b_cs, b_offs], writes=[b_cs])
        hs = [sb(f"a_h{i}", [128, TQ], BF16) for i in range(3)]
        b_hs = [S.buf(f"h{i}") for i in range(3)]
        nhs = [sb(f"a_nh{i}", [128, TQ], BF16) for i in range(3)]
        b_nhs = [S.buf(f"nh{i}") for i in range(3)]
        for i in range(3):
            S.op(DVE, lambda e, i=i: e.tensor_copy(out=hs[i][:], in_=cs[:]), reads=[b_cs], writes=[b_hs[i]])
            if i < 2:
                S.op(DVE, lambda e, i=i: e.tensor_tensor(out=cs[:], in0=cs[:], in1=hs[i][:], op=ALU.subtract),
                     reads=[b_cs, b_hs[i]], writes=[b_cs])
            S.op(DVE, lambda e, i=i: e.tensor_scalar(out=nhs[i][:], in0=hs[i][:], scalar1=-1.0, scalar2=None,
                                                     op0=ALU.mult), reads=[b_hs[i]], writes=[b_nhs[i]])
            for p in range(NT):
                S.dma(SP, lambda e, i=i, p=p: e.dma_start(out=C.KT[0:8, 64 + i, p * TQ:(p + 1) * TQ],
                                                          in_=hs[i][8 * p:8 * p + 8, :]),
                      reads=[b_hs[i]], writes=[C.b_KTc[p]], sem_buf=b_hs[i])
                if p % 2 == 0:
                    j = p // 2
                    S.dma(SP, lambda e, i=i, p=p, j=j: e.dma_start(out=C.QT[0:8, 67 + i, j * TQ:(j + 1) * TQ],
                                                                   in_=nhs[i][8 * p:8 * p + 8, :]),
                          reads=[b_nhs[i]], writes=[C.b_QTc[j]], sem_buf=b_nhs[i])

        g2 = None
        for u in gate_pending:
            if g2 is not None:
                gate2(g2)
            g2 = gate1(*u)
        if g2 is not None:
            gate2(g2)


def _stage_b(C, l):
    nc, S = C.nc, C.S
    FL = ''
    with ExitStack() as st:
        def sb(name, shape, dt):
            return st.enter_context(nc.sbuf_tensor(C.pfx + name, list(shape), dt))

        def ps(name, shape, dt=F32):
            return st.enter_context(nc.psum_tensor(C.pfx + name, list(shape), dt))

        tri = sb("b_tri", [128, 128], BF16); b_tri = S.buf("tri")
        S.dma(SP, lambda e: e.dma_start(out=tri[:], in_=C.tri.ap()), writes=[b_tri], sem_buf=b_tri)
        onesf = sb("b_onesf", [128, 128], BF16); b_onesf = S.buf("onesf")
        S.op(POOL, lambda e: e.memset(onesf[:], 1.0), writes=[b_onesf])
        ktr = [[(sb(f"b_kt{i}_{hd}", [R, SEQ], BF16), [S.buf(f"bkt{i}_{hd}a"), S.buf(f"bkt{i}_{hd}b")]) for hd in range(2)]
               for i in range(2)]
        qtr = [[(sb(f"b_qt{i}_{hd}", [R, NS * TQ], BF16), S.buf(f"bqt{i}_{hd}")) for hd in range(2)] for i in range(2)]
        vvr = [(sb(f"b_vv{i}", [128, SEQ // 128, 192], BF16), [S.buf(f"bvv{i}_{q4}") for q4 in range(4)]) for i in range(2)]
        sps = Ring([(ps(f"b_s{i}", [128, 2 * TQ]), S.buf(f"bs{i}")) for i in range(2)])
        accr = Ring([(ps(f"b_acc{i}", [128, TQ]), S.buf(f"bacc{i}")) for i in range(3)])
        bcp = Ring([(ps("b_bcp", [128, TQ]), S.buf("bbcp"))])
        pr = Ring([(sb(f"b_p{i}", [128, 2 * TQ], BF16), S.buf(f"bp{i}")) for i in range(4)])
        rcr = Ring([(sb(f"b_rc{i}", [128, TQ], F32), S.buf(f"brc{i}")) for i in range(3)])
        r12r = Ring([(sb(f"b_r12{i}", [128, 2, TQ], BF16), S.buf(f"br12{i}")) for i in range(3)])
        bcr = Ring([(sb(f"b_bc{i}", [128, TQ], F32), S.buf(f"bbc{i}")) for i in range(2)])
        osr = Ring([(sb(f"b_o{i}", [128, TQ], BF16), S.buf(f"bo{i}")) for i in range(2)])

        def loads(i):
            for hd in range(2):
                h = 2 * i + hd
                kt_t, b_kt = ktr[i % 2][hd]
                S.dma(SP, lambda e, kt_t=kt_t, h=h: e.dma_start(out=kt_t[:, 0:1024], in_=C.KT[h, :, 0:1024]),
                      reads=[C.b_KT[h]] + (C.b_KTc if h < 8 else []), writes=[b_kt[0]], sem_buf=b_kt[0])
                S.dma(SP, lambda e, kt_t=kt_t, h=h: e.dma_start(out=kt_t[:, 1024:SEQ], in_=C.KT[h, :, 1024:SEQ]),
                      reads=[C.b_KT[h]] + (C.b_KTc if h < 8 else []), writes=[b_kt[1]], sem_buf=b_kt[1])
                qt_t, b_qt = qtr[i % 2][hd]
                S.dma(SP, lambda e, qt_t=qt_t, h=h: e.dma_start(out=qt_t[:], in_=C.QT[h]),
                      reads=[C.b_QT[h]] + (C.b_QTc if h < 8 else []), writes=[b_qt], sem_buf=b_qt)
            vv_t, b_vv = vvr[i % 2]
            for q4 in range(4):
                S.dma(SP, lambda e, vv_t=vv_t, i=i, q4=q4: e.dma_start(
                    out=vv_t[:, q4 * 16:(q4 + 1) * 16, :],
                    in_=C.VP[i, q4 * 2048:(q4 + 1) * 2048, :].rearrange("(k q) c -> q k c", q=128)),
                    reads=[C.b_VP[i]] + C.b_VPs, writes=[b_vv[q4]], sem_buf=b_vv[q4])

        its = []
        for i in range(C.nheadpairs):
            for j in range(C.npairs):
                for hd in range(2):
                    n2 = 4 * j + 4
                    for k2 in range(n2):
                        its.append((i, j, hd, k2, n2))
        state = {}

        def unit(i, j, hd):
            key = (i, j, hd)
            if key not in state:
                if hd == 0:
                    state[("o", i, j)] = osr.next()
                state[key] = accr.next()
            return state[key]

        def col0(it, t):
            i, j, hd, k2, n2 = it
            if n2 - 4 <= k2 < n2 - 2 and 'nocol' not in FL:
                return 128 * (2 * (k2 - (n2 - 4)) + t)
            return 0

        def s_mm(it):
            i, j, hd, k2, n2 = it
            kt_t, b_kt = ktr[i % 2][hd]
            qt_t, b_qt = qtr[i % 2][hd]
            s_t, b_s = sps.next()
            for t in range(2):
                kt = 2 * k2 + t
                c0 = col0(it, t)
                S.op(PE, lambda e, s_t=s_t, kt=kt, t=t, kt_t=kt_t, qt_t=qt_t, j=j, c0=c0: e.matmul(
                    s_t[:, t * TQ + c0:(t + 1) * TQ], lhsT=kt_t[:, kt * 128:(kt + 1) * 128],
                    rhs=qt_t[:, j * TQ + c0:(j + 1) * TQ], start=True, stop=True),
                    reads=[b_kt[0 if kt < 8 else 1], b_qt], writes=[b_s])
            return s_t, b_s

        def tail_a(i, j, hd):
            acc, b_acc = unit(i, j, hd)
            prow = 64 if hd == 0 else 0
            rc_t, b_rc = rcr.next()
            r12_t, b_r12 = r12r.next()
            pr_ = slice(prow, prow + 1)
            S.op(DVE, lambda e: e.tensor_copy(out=r12_t[pr_, 0, :], in_=acc[pr_, :]), reads=[b_acc], writes=[b_r12])
            S.op(DVE, lambda e: e.tensor_tensor(out=rc_t[pr_, :], in0=acc[pr_, :], in1=r12_t[pr_, 0, :], op=ALU.subtract),
                 reads=[b_acc, b_r12], writes=[b_rc])
            S.op(DVE, lambda e: e.tensor_copy(out=r12_t[pr_, 1, :], in_=rc_t[pr_, :]), reads=[b_rc, b_r12], writes=[b_r12])
            return r12_t, b_r12

        def tail_b(i, j, hd, rc):
            r12_t, b_r12 = rc
            acc, b_acc = unit(i, j, hd)
            o_t, b_o = state[("o", i, j)]
            prow = 64 if hd == 0 else 0
            o0 = 0 if hd == 0 else 64
            bp_t, b_bp = bcp.next()
            for k in range(2):
                S.op(PE, lambda e, k=k: e.matmul(
                    bp_t[:], lhsT=onesf[prow:prow + 1, :], rhs=r12_t[prow:prow + 1, k, :], start=(k == 0), stop=(k == 1)),
                    reads=[b_onesf, b_r12], writes=[b_bp])
            bc_t, b_bc = bcr.next()
            S.op(DVE, lambda e: e.reciprocal(out=bc_t[o0:o0 + 64, :], in_=bp_t[o0:o0 + 64, :]), reads=[b_bp], writes=[b_bc])
            S.op(DVE, lambda e: e.tensor_tensor(
                out=o_t[o0:o0 + 64, :], in0=acc[o0:o0 + 64, :], in1=bc_t[o0:o0 + 64, :], op=ALU.mult),
                reads=[b_acc, b_bc], writes=[b_o])
            if hd == 1:
                S.dma(SP, lambda e: e.dma_start(
                    out=C.OT[i, :, j * TQ:(j + 1) * TQ], in_=o_t[:]), reads=[b_o], writes=[C.b_OT[i]], sem_buf=b_o)

        loads(0)
        if C.nheadpairs > 1:
            loads(1)
        QW = DFF // 2
        wsr = Ring([(sb(f"b_ws{i}", [128, QW], F32), S.buf(f"bws{i}")) for i in range(2)])
        wbr = Ring([(sb(f"b_wb{i}", [128, QW], BF16), S.buf(f"bwb{i}")) for i in range(2)])
        precast = []

        def add_chunk(src, dst, n, b_dst, qscale=None, dst_is_3d=False):
            st_ = {}

            def load():
                st_["ws"] = wsr.next()
                ws_t, b_ws = st_["ws"]
                S.dma(SP, lambda e: e.dma_start(out=ws_t[:, 0:n], in_=src), writes=[b_ws], sem_buf=b_ws)

            def cast():
                ws_t, b_ws = st_["ws"]
                st_["wb"] = wbr.next()
                wb_t, b_wb = st_["wb"]
                S.op(POOL, lambda e: e.tensor_copy(out=wb_t[:, 0:n], in_=ws_t[:, 0:n]), reads=[b_ws], writes=[b_wb])
                if qscale is not None:
                    q0, q1 = qscale
                    S.op(POOL, lambda e: e.tensor_scalar(out=wb_t[:, q0:q1], in0=wb_t[:, q0:q1], scalar1=0.125, scalar2=None,
                                                         op0=ALU.mult), reads=[b_wb], writes=[b_wb])

            def store():
                wb_t, b_wb = st_["wb"]
                src_ap = wb_t[:, 0:n].rearrange("p (f k) -> p f k", k=128) if dst_is_3d else wb_t[:, 0:n]
                S.dma(SP, lambda e: e.dma_start(out=dst, in_=src_ap), reads=[b_wb], writes=[b_dst], sem_buf=b_wb)
            precast.append((load, cast, store))

        nfq = QW // 128
        for c in range(8):
            for hf in range(2):
                for qq in range(2):
                    col = hf * DFF + qq * QW
                    add_chunk(C.w_gu[l, c * 128:(c + 1) * 128, col:col + QW],
                              C.WGU[qq * nfq:(qq + 1) * nfq, :, c, hf * 128:(hf + 1) * 128].rearrange("f p k -> p f k"),
                              QW, C.b_WGU, dst_is_3d=True)
        for i8 in range(8):
            add_chunk(C.w_o[l, i8 * 128:(i8 + 1) * 128, :], C.WOb[i8], D, C.b_WOb)
        for f in range(DFF // 128):
            add_chunk(C.w_down[l, f * 128:(f + 1) * 128, :], C.WDb[f], D, C.b_WDb)
        if l + 1 < DEPTH:
            for c in range(8):
                for (c0, c1, qs) in ((0, 1408, (0, 512)), (1408, 2816, (136, 648)), (2816, INC, None)):
                    add_chunk(C.w_in[l + 1, c * 128:(c + 1) * 128, c0:c1], C.WINb[c, :, c0:c1], c1 - c0, C.b_WINb, qs)
        pc_state = {"k": 0}

        def precast_step():
            k = pc_state["k"]
            pc_state["k"] += 1
            if k - 2 >= 0 and k - 2 < len(precast):
                precast[k - 2][2]()
            if k - 1 >= 0 and k - 1 < len(precast):
                precast[k - 1][1]()
            if k < len(precast):
                precast[k][0]()
            return k - 2 >= len(precast) - 1
        q = [s_mm(its[0])]
        if len(its) > 1:
            q.append(s_mm(its[1]))
        pending = []
        for n, it in enumerate(its):
            i, j, hd, k2, n2 = it
            if k2 == 0 and j == 0 and hd == 0 and i >= 1 and i + 1 < C.nheadpairs:
                loads(i + 1)
            s_t, b_s = q.pop(0)
            while pending and pending[0][0] <= n:
                tail_b(*pending.pop(0)[1])
            if n % 16 == 5 and n >= 133 and pc_state["k"] < len(precast) + 2:
                precast_step()
            acc, b_acc = unit(i, j, hd)
            vv_t, b_vv = vvr[i % 2]
            voff = 0 if hd == 0 else 64
            p_t, b_p = pr.next()
            if n2 - 4 <= k2 < n2 - 2:
                ca = col0(it, 0)
                S.op(ACT, lambda e, p_t=p_t, s_t=s_t, ca=ca: e.activation(
                    out=p_t[:].rearrange("p (t c) -> p t c", t=2)[:, :, ca:TQ],
                    in_=s_t[:].rearrange("p (t c) -> p t c", t=2)[:, :, ca:TQ], func=AF.Exp, bias=EXP_SHIFT),
                    reads=[b_s], writes=[b_p])
            else:
                S.op(ACT, lambda e, p_t=p_t, s_t=s_t: e.activation(out=p_t[:], in_=s_t[:], func=AF.Exp, bias=EXP_SHIFT),
                     reads=[b_s], writes=[b_p])
            if n + 2 < len(its):
                q.append(s_mm(its[n + 2]))
            for t in range(2):
                c0 = col0(it, t)
                if n2 - 4 <= k2 < n2 - 2 and 'notri' not in FL:
                    S.op(POOL, lambda e, p_t=p_t, t=t, c0=c0: e.tensor_tensor(
                        out=p_t[:, t * TQ + c0:t * TQ + c0 + 128], in0=p_t[:, t * TQ + c0:t * TQ + c0 + 128],
                        in1=tri[:], op=ALU.mult), reads=[b_p, b_tri], writes=[b_p])
            for t in range(2):
                kt = 2 * k2 + t
                c0 = col0(it, t)
                S.op(PE, lambda e, acc=acc, vv_t=vv_t, kt=kt, t=t, voff=voff, p_t=p_t, k2=k2, n2=n2, c0=c0: e.matmul(
                    acc[:, c0:TQ], lhsT=vv_t[:, kt, voff:voff + 128], rhs=p_t[:, t * TQ + c0:(t + 1) * TQ],
                    start=(k2 == 0 and t == 0), stop=(k2 == n2 - 1 and t == 1)), reads=[b_vv[kt // 16], b_p], writes=[b_acc])
            if k2 == n2 - 1:
                rc = tail_a(i, j, hd)
                pending.append((n + 5, (i, j, hd, rc)))
        while pending:
            tail_b(*pending.pop(0)[1])
        while pc_state["k"] < len(precast) + 2:
            precast_step()


def stage_a(C, l, xg, skip=()):
    _stage_a(C, l, xg, skip)
    C.S.barrier()


def stage_b(C, l):
    _stage_b(C, l)
    C.S.barrier()


def stage_c(C, l, xg, xout, after_slot=None):
    _stage_c(C, l, xg, xout, after_slot)
    C.S.barrier()


def _stage_c(C, l, xg, xout, after_slot=None):
    nc, S = C.nc, C.S
    NF = DFF // 128
    with ExitStack() as st:
        def sb(name, shape, dt):
            return st.enter_context(nc.sbuf_tensor(C.pfx + name, list(shape), dt))

        def ps(name, shape, dt=F32):
            return st.enter_context(nc.psum_tensor(C.pfx + name, list(shape), dt))

        ident = sb("c_ident", [128, 128], BF16); b_ident = S.buf("cident")
        S.dma(SP, lambda e: e.dma_start(out=ident[:], in_=C.ident.ap()), writes=[b_ident], sem_buf=b_ident)
        wo = sb("c_wo", [128, 8, D], BF16); b_wo = S.buf("wo")
        wd = sb("c_wd", [128, NF, D], BF16); b_wd = [S.buf(f"wd{f}") for f in range(NF)]
        S.dma(SP, lambda e: e.dma_start(out=wo[:], in_=C.WOb.ap().rearrange("i p d -> p i d")),
              reads=[C.b_WOb], writes=[b_wo], sem_buf=b_wo)
        lnt = []
        for nm, src in (("g1", C.ln1_g), ("b1", C.ln1_b), ("g2", C.ln2_g), ("b2", C.ln2_b)):
            t = sb("c_ln" + nm, [128, D], F32); b = S.buf("ln" + nm)
            ap = bass.AP(tensor=src, offset=l * D, ap=[[0, 128], [1, D]])
            S.dma(SP, lambda e, t=t, ap=ap: e.dma_start(out=t[:], in_=ap), writes=[b], sem_buf=b)
            lnt.append((t, b))
        (g1, b_g1), (b1, b_b1), (g2, b_g2), (b2, b_b2) = lnt

        otr = Ring([(sb(f"c_ot{i}", [128, 8, TQ], BF16), S.buf(f"cot{i}")) for i in range(2)])
        xr = Ring([(sb(f"c_x{i}", [128, D], F32), S.buf(f"cx{i}")) for i in range(2)])
        x1s = [sb(f"c_xone{k}", [128, 4, D], F32) for k in range(2)]
        b_x1s = [[S.buf(f"cx1_{k}_{s}") for s in range(4)] for k in range(2)]
        rr_ = Ring([(sb(f"c_r{i}", [128, D], F32), S.buf(f"cr{i}")) for i in range(2)])
        x1br = Ring([(sb(f"c_x1b{i}", [128, D], BF16), S.buf(f"cx1b{i}")) for i in range(2)])
        x1Ts = [sb(f"c_x1T{k}", [128, 8, TQ], BF16) for k in range(2)]
        b_x1Ts = [[S.buf(f"cx1T{k}_{s}") for s in range(4)] for k in range(2)]
        hT = sb("c_hT", [128, NF, TQ], BF16); b_hT = [S.buf(f"chT{f}") for f in range(NF)]
        wgr = Ring([(sb(f"c_wg{i}", [128, 8, 256], BF16), S.buf(f"cwg{i}")) for i in range(3)])
        sgr = Ring([(sb(f"c_sg{i}", [128, TQ], F32), S.buf(f"csg{i}")) for i in range(2)])
        yr = Ring([(sb(f"c_y{i}", [128, D], F32), S.buf(f"cy{i}")) for i in range(2)])
        str_ = Ring([(sb(f"c_st{i}", [128, 2, 6], F32), S.buf(f"cst{i}")) for i in range(2)])
        mvr = Ring([(sb(f"c_mv{i}", [128, 8], F32), S.buf(f"cmv{i}")) for i in range(2)])
        bigr = Ring([(ps(f"c_big{i}", [128, TQ]), S.buf(f"cbig{i}")) for i in range(6)])
        tpr = Ring([(ps(f"c_tp{i}", [128, TQ], BF16), S.buf(f"ctp{i}")) for i in range(2)])

        epst = sb("c_eps", [128, 1], F32); b_epst = S.buf("ceps")
        S.op(POOL, lambda e: e.memset(epst[:], EPS), writes=[b_epst])

        def layer_norm(src_t, b_src, dst_ap, b_dst, gt, b_gt, bt, b_bt):
            st_t, b_st = str_.next()
            for k in range(2):
                S.op(DVE, lambda e, st_t=st_t, k=k: e.bn_stats(out=st_t[:, k, :], in_=src_t[:, k * 512:(k + 1) * 512]),
                     reads=[b_src], writes=[b_st])
            mv_t, b_mv = mvr.next()
            S.op(DVE, lambda e, mv_t=mv_t, st_t=st_t: e.bn_aggr(out=mv_t[:, 0:2], in_=st_t[:].rearrange("p a b -> p (a b)")),
                 reads=[b_st], writes=[b_mv])
            S.op(DVE, lambda e, mv_t=mv_t: e.tensor_scalar(out=mv_t[:, 3:4], in0=mv_t[:, 1:2], scalar1=EPS, scalar2=-0.5,
                                                           op0=ALU.add, op1=ALU.mult), reads=[b_mv], writes=[b_mv])
            S.op(DVE, lambda e, mv_t=mv_t: e.tensor_scalar(out=mv_t[:, 4:5], in0=mv_t[:, 3:4], scalar1=-1.0, scalar2=0.5,
                                                           op0=ALU.mult, op1=ALU.add), reads=[b_mv], writes=[b_mv])
            S.op(DVE, lambda e, mv_t=mv_t: e.reciprocal(out=mv_t[:, 2:3], in_=mv_t[:, 4:5]), reads=[b_mv], writes=[b_mv])
            for it_ in range(7):
                S.op(DVE, lambda e, mv_t=mv_t: e.scalar_tensor_tensor(
                    out=mv_t[:, 5:6], in0=mv_t[:, 2:3], scalar=mv_t[:, 2:3], in1=mv_t[:, 3:4], op0=ALU.mult, op1=ALU.mult),
                    reads=[b_mv], writes=[b_mv])
                S.op(DVE, lambda e, mv_t=mv_t: e.scalar_tensor_tensor(
                    out=mv_t[:, 2:3], in0=mv_t[:, 5:6], scalar=1.5, in1=mv_t[:, 2:3], op0=ALU.add, op1=ALU.mult),
                    reads=[b_mv], writes=[b_mv])
            S.op(DVE, lambda e, mv_t=mv_t: e.tensor_scalar(out=src_t[:], in0=src_t[:], scalar1=mv_t[:, 0:1],
                                                           scalar2=mv_t[:, 2:3], op0=ALU.subtract, op1=ALU.mult),
                 reads=[b_src, b_mv], writes=[b_src])
            S.op(POOL, lambda e: e.tensor_tensor(out=src_t[:], in0=src_t[:], in1=gt[:], op=ALU.mult),
                 reads=[b_src, b_gt], writes=[b_src])
            S.op(POOL, lambda e: e.tensor_tensor(out=dst_ap, in0=src_t[:], in1=bt[:], op=ALU.add),
                 reads=[b_src, b_bt], writes=[b_dst])

        ots = {}

        def p1a(j, s):
            x1, b_x1 = x1s[j % 2], b_x1s[j % 2]
            if s == 0:
                ot_t, b_ot = otr.next()
                S.dma(SP, lambda e: e.dma_start(
                    out=ot_t[:], in_=C.OT[:, :, j * TQ:(j + 1) * TQ].rearrange("i p t -> p i t")),
                    reads=C.b_OT, writes=[b_ot], sem_buf=b_ot)
                ots[j] = (ot_t, b_ot)
            ot_t, b_ot = ots[j]
            x_t, b_x = xr.next()
            xa, xa_bufs = xg(j, s)
            S.dma(SP, lambda e: e.dma_start(out=x_t[:], in_=xa), reads=xa_bufs, writes=[b_x], sem_buf=b_x)
            r_t, b_r = rr_.next()
            for hf in range(2):
                pt, pb = bigr.next()
                for i in range(8):
                    S.op(PE, lambda e, pt=pt, i=i, hf=hf: e.matmul(
                        pt[:], lhsT=ot_t[:, i, s * 128:(s + 1) * 128], rhs=wo[:, i, hf * 512:(hf + 1) * 512],
                        start=(i == 0), stop=(i == 7)), reads=[b_ot, b_wo], writes=[pb])
                S.op(DVE, lambda e, pt=pt, hf=hf: e.scalar_tensor_tensor(
                    out=r_t[:, hf * 512:(hf + 1) * 512], in0=x_t[:, hf * 512:(hf + 1) * 512], scalar=ALPHA,
                    in1=pt[:], op0=ALU.mult, op1=ALU.add), reads=[b_x, pb], writes=[b_r])
            layer_norm(r_t, b_r, x1[:, s, :], b_x1[s], g1, b_g1, b1, b_b1)

        def p1b(j, s):
            x1, b_x1 = x1s[j % 2], b_x1s[j % 2]
            x1T, b_x1T = x1Ts[j % 2], b_x1Ts[j % 2]
            xb_t, b_xb = x1br.next()
            S.op(ACT, lambda e: e.copy(out=xb_t[:], in_=x1[:, s, :]), reads=[b_x1[s]], writes=[b_xb])
            for cg in range(2):
                tp, b_tp = tpr.next()
                for k in range(4):
                    cc = 4 * cg + k
                    S.op(PE, lambda e, tp=tp, k=k, cc=cc: e.transpose(
                        tp[:, k * 128:(k + 1) * 128], xb_t[:, cc * 128:(cc + 1) * 128], ident[:]),
                        reads=[b_xb, b_ident], writes=[b_tp])
                S.op(ACT, lambda e, tp=tp, cg=cg: e.copy(
                    out=x1T[:, 4 * cg:4 * cg + 4, s * 128:(s + 1) * 128],
                    in_=tp[:].rearrange("p (k t) -> p k t", k=4)), reads=[b_tp], writes=[b_x1T[s]])

        wg_issued = [0]
        wg_q = []
        ystore = []
        for s in range(4):
            p1a(0, s)
        for s in range(4):
            p1b(0, s)
        for k in range(2):
            S.dma(SP, lambda e, k=k: e.dma_start(out=wd[:, 11 * k:11 * k + 11, :],
                                                 in_=C.WDb[11 * k:11 * k + 11].rearrange("f p d -> p f d")),
                  reads=[C.b_WDb], writes=b_wd[11 * k:11 * k + 11], sem_buf=b_wd[11 * k])
        for j in range(C.npairs):
            x1, b_x1 = x1s[j % 2], b_x1s[j % 2]
            x1T, b_x1T = x1Ts[j % 2], b_x1Ts[j % 2]
            nxt = j + 1 < C.npairs
            for f in range(NF):
                if f == 4 and ystore:
                    ystore.pop(0)()
                if nxt and f % 5 == 1 and f // 5 < 4:
                    p1a(j + 1, f // 5)
                if nxt and f >= 6 and (f - 6) % 5 == 0 and (f - 6) // 5 < 3:
                    p1b(j + 1, (f - 6) // 5)
                g_idx = j * NF + f
                while wg_issued[0] < min(g_idx + 3, C.npairs * NF):
                    gi_ = wg_issued[0]
                    wt, wb_ = wgr.next()
                    S.dma(SP, lambda e, wt=wt, ff=gi_ % NF: e.dma_start(out=wt[:], in_=C.WGU[ff]),
                          reads=[C.b_WGU], writes=[wb_], sem_buf=wb_)
                    wg_q.append((wt, wb_))
                    wg_issued[0] += 1
                wg_t, b_wg = wg_q.pop(0)
                gp, b_gp = bigr.next()
                up, b_up = bigr.next()
                for (pp, bb, o) in ((gp, b_gp, 0), (up, b_up, 128)):
                    for cc in range(8):
                        S.op(PE, lambda e, pp=pp, wg_t=wg_t, cc=cc, o=o, x1T=x1T: e.matmul(
                            pp[:], lhsT=wg_t[:, cc, o:o + 128], rhs=x1T[:, cc, :], start=(cc == 0), stop=(cc == 7)),
                            reads=[b_wg] + b_x1T, writes=[bb])
                sg_t, b_sg = sgr.next()
                S.op(ACT, lambda e, sg_t=sg_t, gp=gp: e.activation(out=sg_t[:], in_=gp[:], func=AF.Silu),
                     reads=[b_gp], writes=[b_sg])
                S.op(DVE, lambda e, sg_t=sg_t, up=up, f=f: e.tensor_tensor(
                    out=hT[:, f, :], in0=sg_t[:], in1=up[:], op=ALU.mult), reads=[b_sg, b_up], writes=[b_hT[f]])
            for s in range(4):
                if nxt and s == 1:
                    p1b(j + 1, 3)
                r_t, b_r = rr_.next()
                for hf in range(2):
                    pt, pb = bigr.next()
                    for f in range(NF):
                        S.op(PE, lambda e, pt=pt, f=f, s=s, hf=hf: e.matmul(
                            pt[:], lhsT=hT[:, f, s * 128:(s + 1) * 128], rhs=wd[:, f, hf * 512:(hf + 1) * 512],
                            start=(f == 0), stop=(f == NF - 1)), reads=[b_hT[f], b_wd[f]], writes=[pb])
                    S.op(DVE, lambda e, r_t=r_t, pt=pt, hf=hf, s=s, x1=x1: e.scalar_tensor_tensor(
                        out=r_t[:, hf * 512:(hf + 1) * 512], in0=x1[:, s, hf * 512:(hf + 1) * 512], scalar=ALPHA,
                        in1=pt[:], op0=ALU.mult, op1=ALU.add), reads=[b_x1[s], pb], writes=[b_r])
                if ystore:
                    ystore.pop(0)()
                y_t, b_y = yr.next()
                layer_norm(r_t, b_r, y_t[:], b_y, g2, b_g2, b2, b_b2)

                def st_fn(y_t=y_t, b_y=b_y, j=j, s=s):
                    S.dma(SP, lambda e: e.dma_start(out=xout[0][j, s * 128:(s + 1) * 128, :], in_=y_t[:]),
                          reads=[b_y], writes=xout[1](j), sem_buf=b_y)
                    if s == 3 and after_slot is not None:
                        after_slot(j)
                ystore.append(st_fn)
        while ystore:
            ystore.pop(0)()


def build_program(debug=False, skip=()):
    import os
    nc = bass.Bass("TRN2", target_bir_lowering=False)
    C = Ctx()
    C.nc = nc
    C.dbg = None
    C.npairs = NS
    C.nheadpairs = 8
    C.xg = dram_in(nc, "xg", [NT, TQ, D])
    C.w_in = dram_in(nc, "w_in", [DEPTH, D, INC])
    C.b_f = dram_in(nc, "b_f", [DEPTH, 8])
    C.w_o = dram_in(nc, "w_o", [DEPTH, D, D])
    C.w_gu = dram_in(nc, "w_gu", [DEPTH, D, 2 * DFF])
    C.w_down = dram_in(nc, "w_down", [DEPTH, DFF, D])
    C.ln1_g = dram_in(nc, "ln1_g", [DEPTH, D])
    C.ln1_b = dram_in(nc, "ln1_b", [DEPTH, D])
    C.ln2_g = dram_in(nc, "ln2_g", [DEPTH, D])
    C.ln2_b = dram_in(nc, "ln2_b", [DEPTH, D])
    C.kc = dram_in(nc, "kc", [NH, NCR, SEQ], BF16)
    C.qc = dram_in(nc, "qc", [NH, NCR, NS * TQ], BF16)
    C.masks = dram_in(nc, "masks", [128, 8, TQ], BF16)
    C.gbias = dram_in(nc, "gbias", [128, NS, 4, 32])
    C.ownm = dram_in(nc, "ownm", [128, NS, 4, 32])
    C.mbig = dram_in(nc, "mbig", [128, 128])
    C.ident = dram_in(nc, "ident", [128, 128], BF16)
    C.gsel = dram_in(nc, "gsel", [128, 2])
    C.tri = dram_in(nc, "tri", [128, 128], BF16)
    C.KT = nc.dram_tensor("KT", [NH, R, SEQ], BF16)
    C.QT = nc.dram_tensor("QT", [NH, R, NS * TQ], BF16)
    C.VP = nc.dram_tensor("VP", [8, SEQ, 192], BF16)
    C.OT = nc.dram_tensor("OT", [8, 128, NS * TQ], BF16)
    C.WGU = nc.dram_tensor("WGU", [DFF // 128, 128, 8, 256], BF16)
    C.WOb = nc.dram_tensor("WOb", [8, 128, D], BF16)
    C.WDb = nc.dram_tensor("WDb", [DFF // 128, 128, D], BF16)
    C.WINb = nc.dram_tensor("WINb", [8, 128, INC], BF16)
    C.x2own = nc.dram_tensor("x2own", [NS, TQ, D], F32)
    C.x2all = nc.dram_tensor("x2all", [NS, 2 * TQ, D], F32)
    C.xout = nc.dram_tensor("xout", [NS, TQ, D], F32, kind="ExternalOutput")
    with ExitStack() as st:
        S = Sched(nc, st)
        C.S = S
        C.b_KT = [S.buf(f"KT{h}") for h in range(NH)]
        C.b_QT = [S.buf(f"QT{h}") for h in range(NH)]
        C.b_VP = [S.buf(f"VP{i}") for i in range(8)]
        C.b_VPs = [S.buf(f"VPs{i}") for i in range(4)]
        C.b_KTc = [S.buf(f"KTc{p}") for p in range(NT)]
        C.b_QTc = [S.buf(f"QTc{j}") for j in range(NS)]
        C.b_OT = [S.buf(f"OT{i}") for i in range(8)]
        C.b_WGU = S.buf("WGU")
        C.b_WOb = S.buf("WOb")
        C.b_WDb = S.buf("WDb")
        C.b_WINb = S.buf("WINb")
        b_x2own = [S.buf(f"x2own{j}") for j in range(NS)]
        b_x2all = [S.buf(f"x2all{j}") for j in range(NS)]
        b_cc = S.buf("constcopy")
        for h in range(NH):
            S.dma(SP, lambda e, h=h: e.dma_start(out=C.KT[h, 64:R, :], in_=C.kc[h]), writes=[C.b_KT[h]], sem_buf=b_cc)
            S.dma(SP, lambda e, h=h: e.dma_start(out=C.QT[h, 64:R, :], in_=C.qc[h]), writes=[C.b_QT[h]], sem_buf=b_cc)
        C.pfx = S.pfx = "L0_"
        C.blend = False
        stage_a(C, 0, lambda p: ([C.xg[p]], []), skip=skip)
        stage_b(C, 0)
        b_coll = S.buf("coll")

        def exchange(j):
            S.dma(POOL, lambda e, j=j: e.collective_compute(
                "AllGather", ALU.bypass, replica_groups=[[0, 1], [2, 3], [4, 5], [6, 7]],
                ins=[C.x2own[j]], outs=[C.x2all[j]]),
                reads=[b_x2own[j]], writes=[b_x2all[j]], sem_buf=b_coll, inc=1)
        stage_c(C, 0, lambda j, s: (C.xg[2 * j, s * 128:(s + 1) * 128, :], []), (C.x2own, lambda j: [b_x2own[j]]),
                after_slot=exchange)
        C.pfx = S.pfx = "L1_"
        C.blend = True

        def xsrc1(p):
            j = p // 2
            if p % 2 == 0:
                return [C.x2own[j]], [b_x2own[j]]
            return [C.x2all[j, 0:TQ, :], C.x2all[j, TQ:2 * TQ, :]], [b_x2all[j]]
        stage_a(C, 1, xsrc1, skip=skip)
        stage_b(C, 1)
        stage_c(C, 1, lambda j, s: (C.x2own[j, s * 128:(s + 1) * 128, :], [b_x2own[j]]), (C.xout, lambda j: []))
        print(S.emit())
    return nc


_NC = {}


def kernel(x, w_in, b_f, w_o, ln1_g, ln1_b, w_gu, w_down, ln2_g, ln2_b):
    x = np.asarray(x, np.float32)
    if "nc" not in _NC:
        _NC["nc"] = build_program()
    nc = _NC["nc"]
    f32c = lambda a: np.ascontiguousarray(np.asarray(a, np.float32))
    shared = dict(w_in=f32c(w_in), b_f=f32c(b_f), w_o=f32c(w_o), w_gu=f32c(w_gu), w_down=f32c(w_down),
                  ln1_g=f32c(ln1_g), ln1_b=f32c(ln1_b), ln2_g=f32c(ln2_g), ln2_b=f32c(ln2_b))
    in_maps = []
    for c in range(8):
        b, g = c // 2, c % 2
        t = host_tables(g)
        xg = np.ascontiguousarray(x[b].reshape(NT, TQ, D)[[true_tile(p, g) for p in range(NT)]])
        gsel = np.zeros((128, 2), np.float32)
        gsel[:, 0] = float(g)
        gsel[:, 1] = float(1 - g)
        in_maps.append(dict(xg=xg, gsel=gsel, **shared, **t))
    res = run_bass_kernel_spmd(nc, in_maps, core_ids=list(range(8)))
    out = np.empty_like(x)
    for c in range(8):
        b, g = c // 2, c % 2
        xo = np.asarray(res.results[c]["xout"], np.float32)
        for j in range(NS):
            T = 2 * j + g
            out[b, T * TQ:(T + 1) * TQ] = xo[j]
    return out
```
